# Optimizing a Trainium2 kernel written in Bass

```python
import math
import jax, jax.numpy as jnp
from jax import lax
import numpy as np

D_MODEL = 1024
BATCH = 4
SEQ = 4096
DEPTH = 2

GRID_W = 64
CTX_LEN = 256
EPS = 1e-6
POOL_WINDOWS = (2, 4, 8, 16)
POOL_GROUP = D_MODEL // 16
POOL_WIDTH = len(POOL_WINDOWS) * POOL_GROUP
DIFF_HEAD_DIM = 64
DIFF_V_DIM = 2 * DIFF_HEAD_DIM
DIFF_HEADS = (D_MODEL - POOL_WIDTH) // DIFF_V_DIM
DIFF_WIDTH = DIFF_HEADS * DIFF_V_DIM
QK_WIDTH = DIFF_HEADS * 2 * DIFF_HEAD_DIM
EVEN_IN_WIDTH = POOL_WIDTH + 2 * QK_WIDTH + DIFF_WIDTH
SPLIT_IDX = [POOL_WIDTH, POOL_WIDTH + QK_WIDTH, POOL_WIDTH + 2 * QK_WIDTH]
KV_START = POOL_WIDTH + QK_WIDTH
Q_BLOCK = 128
ROPE_THETA = 10000.0
ROPE_AXIS_PAIRS = DIFF_HEAD_DIM // 4
CHUNK = 128
SG_WIDTH = 2 * D_MODEL
SG_GROUPS = 8
SG_GROUP_DIM = SG_WIDTH // SG_GROUPS
FFN_HIDDEN = 4 * D_MODEL

kernel_name = 'hybrid_pool_diffattn_sgmlp_prefix_dit'


def rms_norm(x, g):
    xf = x.astype(jnp.float32)
    y = xf * lax.rsqrt(jnp.mean(xf * xf, axis=-1, keepdims=True) + EPS) * g
    return y.astype(x.dtype)


def layer_norm(x, g, b):
    xf = x.astype(jnp.float32)
    mu = jnp.mean(xf, axis=-1, keepdims=True)
    var = jnp.mean(jnp.square(xf - mu), axis=-1, keepdims=True)
    return ((xf - mu) * lax.rsqrt(var + EPS) * g + b).astype(x.dtype)


def modulate(h, shift, scale):
    return h * (1.0 + scale) + shift


def axial_rope(n_tok):
    t = jnp.arange(n_tok, dtype=jnp.int32)
    rows = (t // GRID_W).astype(jnp.float32)
    cols = (t % GRID_W).astype(jnp.float32)
    inv = ROPE_THETA ** (-jnp.arange(ROPE_AXIS_PAIRS, dtype=jnp.float32) / ROPE_AXIS_PAIRS)
    ang = jnp.concatenate([rows[:, None] * inv, cols[:, None] * inv], axis=-1)
    return jnp.cos(ang), jnp.sin(ang)


def apply_rope(x, cos, sin):
    shp = (1, cos.shape[0]) + (1,) * (x.ndim - 3) + (cos.shape[1],)
    cs, sn = cos.reshape(shp), sin.reshape(shp)
    xr = x.astype(jnp.float32).reshape(x.shape[:-1] + (-1, 2))
    x0, x1 = xr[..., 0], xr[..., 1]
    out = jnp.stack([x0 * cs - x1 * sn, x0 * sn + x1 * cs], axis=-1)
    return out.reshape(x.shape).astype(x.dtype)


def heads_qk(t, g):
    t = t.reshape(t.shape[:2] + (DIFF_HEADS, 2, DIFF_HEAD_DIM))
    return rms_norm(t, g)


def pool_mix(u, pool_w, pool_scale):
    b_, n, _ = u.shape
    uf = u.astype(jnp.float32)
    csum = jnp.concatenate([jnp.zeros((b_, 1, POOL_WIDTH), jnp.float32), jnp.cumsum(uf, axis=1)], axis=1)
    t = jnp.arange(n)
    outs = []
    for gi, w in enumerate(POOL_WINDOWS):
        lo = jnp.clip(t - w // 2, 0, n)
        hi = jnp.clip(t + w // 2, 0, n)
        sl = slice(gi * POOL_GROUP, (gi + 1) * POOL_GROUP)
        cs = csum[:, :, sl]
        mean = (cs[:, hi] - cs[:, lo]) / (hi - lo).astype(jnp.float32)[None, :, None]
        outs.append(mean - uf[:, :, sl])
    pooled = jnp.stack(outs, axis=2).astype(u.dtype)
    y = jnp.einsum('blgc,gcd->blgd', pooled, pool_w).reshape(b_, n, POOL_WIDTH)
    return y * pool_scale


def diff_attention(q, k, v, lam):
    b_, n = q.shape[:2]
    nb = n // Q_BLOCK
    qb = jnp.moveaxis(q.reshape((b_, nb, Q_BLOCK) + q.shape[2:]), 1, 0)
    scale = DIFF_HEAD_DIM ** -0.5

    def block(qi):
        s = jnp.einsum('bqhcd,bkhcd->bchqk', qi, k, preferred_element_type=jnp.float32) * scale
        p = jax.nn.softmax(s, axis=-1)
        wgt = p[:, 0] - lam * p[:, 1]
        return jnp.einsum('bhqk,bkhe->bqhe', wgt.astype(v.dtype), v)

    out = lax.map(block, qb)
    return jnp.moveaxis(out, 0, 1).reshape(b_, n, DIFF_HEADS, DIFF_V_DIM)


def even_merge(pool_in, attn, w_out, pool_w, pool_scale, sub_g, lam_init):
    attn = rms_norm(attn, sub_g) * (1.0 - lam_init)
    y = jnp.concatenate([pool_mix(pool_in, pool_w, pool_scale),
                         attn.reshape(attn.shape[:2] + (DIFF_WIDTH,))], axis=-1)
    return y @ w_out


def even_mixer(h, hc, w_in, w_out, pool_w, pool_scale, q_g, k_g, lam, sub_g, lam_init, cos, sin, ctx_out):
    b_, n, _ = h.shape
    pool_in, q, k, v = jnp.split(h @ w_in, SPLIT_IDX, axis=-1)
    q = apply_rope(heads_qk(q, q_g), cos, sin)
    k = apply_rope(heads_qk(k, k_g), cos, sin)
    v = v.reshape(b_, n, DIFF_HEADS, DIFF_V_DIM)
    if ctx_out:
        pool_c, qc, kc, vc = jnp.split(hc @ w_in, SPLIT_IDX, axis=-1)
    else:
        kc, vc = jnp.split(hc @ w_in[:, KV_START:], [QK_WIDTH], axis=-1)
    kc = heads_qk(kc, k_g)
    vc = vc.reshape(b_, -1, DIFF_HEADS, DIFF_V_DIM)
    k_all = jnp.concatenate([kc, k], axis=1)
    v_all = jnp.concatenate([vc, v], axis=1)
    y = even_merge(pool_in, diff_attention(q, k_all, v_all, lam), w_out, pool_w, pool_scale, sub_g, lam_init)
    yc = None
    if ctx_out:
        attn_c = diff_attention(heads_qk(qc, q_g), kc, vc, lam)
        yc = even_merge(pool_c, attn_c, w_out, pool_w, pool_scale, sub_g, lam_init)
    return y, yc


def spatial_gating(h, w_in, ln_g, ln_b, sg_w, sg_b, w_out):
    b_, n, _ = h.shape
    u, v = jnp.split(jax.nn.gelu(h @ w_in, approximate=False), 2, axis=-1)
    v = layer_norm(v, ln_g, ln_b)
    vc = v.reshape(b_, n // CHUNK, CHUNK, SG_GROUPS, SG_GROUP_DIM)
    mixed = jnp.einsum('gmn,bkngc->bkmgc', sg_w, vc) + jnp.swapaxes(sg_b, 0, 1)[:, :, None]
    return (u * mixed.reshape(b_, n, SG_WIDTH)) @ w_out


def sq_relu_ffn(h, w1, w2):
    return jnp.square(jax.nn.relu(h @ w1)) @ w2


def setup_inputs(seed: int = 0) -> dict:
    key = jax.random.key(seed)
    ks = iter(jax.random.split(key, 32))
    D = D_MODEL
    ne, no = (DEPTH + 1) // 2, DEPTH // 2

    def nrm(shape, s):
        return jax.random.normal(next(ks), shape, jnp.float32) * s

    return {
        'x': nrm((BATCH, SEQ, D), 1.0),
        'c': nrm((BATCH, D), 1.0),
        'ctx': nrm((BATCH, CTX_LEN, D), 1.0),
        'c_ctx': nrm((D,), 1.0),
        'ada_w': nrm((DEPTH, D, 6 * D), D ** -0.5),
        'ada_b': nrm((DEPTH, 6 * D), 0.02),
        'norm_mix_g': 1.0 + nrm((DEPTH, D), 0.02),
        'norm_ffn_g': 1.0 + nrm((DEPTH, D), 0.02),
        'ffn_w1': nrm((DEPTH, D, FFN_HIDDEN), D ** -0.5),
        'ffn_w2': nrm((DEPTH, FFN_HIDDEN, D), FFN_HIDDEN ** -0.5),
        'ev_w_in': nrm((ne, D, EVEN_IN_WIDTH), D ** -0.5),
        'ev_w_out': nrm((ne, POOL_WIDTH + DIFF_WIDTH, D), (POOL_WIDTH + DIFF_WIDTH) ** -0.5),
        'pool_w': nrm((ne, len(POOL_WINDOWS), POOL_GROUP, POOL_GROUP), POOL_GROUP ** -0.5),
        'pool_scale': 1.0 + nrm((ne, POOL_WIDTH), 0.02),
        'q_norm_g': 1.0 + nrm((ne, DIFF_HEAD_DIM), 0.02),
        'k_norm_g': 1.0 + nrm((ne, DIFF_HEAD_DIM), 0.02),
        'lam_q1': nrm((ne, DIFF_HEAD_DIM), 0.1),
        'lam_k1': nrm((ne, DIFF_HEAD_DIM), 0.1),
        'lam_q2': nrm((ne, DIFF_HEAD_DIM), 0.1),
        'lam_k2': nrm((ne, DIFF_HEAD_DIM), 0.1),
        'sub_norm_g': 1.0 + nrm((ne, DIFF_V_DIM), 0.02),
        'od_w_in': nrm((no, D, 2 * SG_WIDTH), D ** -0.5),
        'sg_ln_g': 1.0 + nrm((no, SG_WIDTH), 0.02),
        'sg_ln_b': nrm((no, SG_WIDTH), 0.02),
        'sg_w': nrm((no, SG_GROUPS, CHUNK, CHUNK), CHUNK ** -0.5),
        'sg_b': 1.0 + nrm((no, SG_GROUPS, CHUNK), 0.02),
        'od_w_out': nrm((no, SG_WIDTH, D), SG_WIDTH ** -0.5),
    }


def reference(x, c, ctx, c_ctx, ada_w, ada_b, norm_mix_g, norm_ffn_g, ffn_w1, ffn_w2,
              ev_w_in, ev_w_out, pool_w, pool_scale, q_norm_g, k_norm_g,
              lam_q1, lam_k1, lam_q2, lam_k2, sub_norm_g,
              od_w_in, sg_ln_g, sg_ln_b, sg_w, sg_b, od_w_out):
    n_lat = x.shape[1]
    cos, sin = axial_rope(n_lat)
    sc = jax.nn.silu(c)
    scc = jax.nn.silu(c_ctx)
    xc = ctx
    for l in range(DEPTH):
        ctx_out = any(j % 2 == 0 for j in range(l + 1, DEPTH))
        uses_ctx = (l % 2 == 0) or ctx_out
        sh1, sc1, g1, sh2, sc2, g2 = jnp.split((sc @ ada_w[l] + ada_b[l])[:, None, :], 6, axis=-1)
        h = modulate(rms_norm(x, norm_mix_g[l]), sh1, sc1)
        hc = None
        if uses_ctx:
            csh1, csc1, cg1, csh2, csc2, cg2 = jnp.split(scc @ ada_w[l] + ada_b[l], 6)
            hc = modulate(rms_norm(xc, norm_mix_g[l]), csh1, csc1)
        if l % 2 == 0:
            e = l // 2
            lam_init = 0.8 - 0.6 * math.exp(-0.3 * l)
            lam = (jnp.exp(jnp.sum(lam_q1[e].astype(jnp.float32) * lam_k1[e].astype(jnp.float32)))
                   - jnp.exp(jnp.sum(lam_q2[e].astype(jnp.float32) * lam_k2[e].astype(jnp.float32)))
                   + lam_init)
            y, yc = even_mixer(h, hc, ev_w_in[e], ev_w_out[e], pool_w[e], pool_scale[e],
                               q_norm_g[e], k_norm_g[e], lam, sub_norm_g[e], lam_init, cos, sin, ctx_out)
        else:
            o = l // 2
            y = spatial_gating(h, od_w_in[o], sg_ln_g[o], sg_ln_b[o], sg_w[o], sg_b[o], od_w_out[o])
            yc = None
            if ctx_out:
                yc = spatial_gating(hc, od_w_in[o], sg_ln_g[o], sg_ln_b[o], sg_w[o], sg_b[o], od_w_out[o])
        x = x + g1 * y
        x = x + g2 * sq_relu_ffn(modulate(rms_norm(x, norm_ffn_g[l]), sh2, sc2), ffn_w1[l], ffn_w2[l])
        if ctx_out:
            xc = xc + cg1 * yc
            xc = xc + cg2 * sq_relu_ffn(modulate(rms_norm(xc, norm_ffn_g[l]), csh2, csc2), ffn_w1[l], ffn_w2[l])
    return x
```

```python
import math
import numpy as np
from contextlib import ExitStack
import concourse.bass as bass
import concourse.mybir as mybir
from concourse.bass_utils import run_bass_kernel_spmd

F32 = mybir.dt.float32
BF16 = mybir.dt.bfloat16
AF = mybir.ActivationFunctionType
ALU = mybir.AluOpType
AX = mybir.AxisListType

DEBUG = False
POOL = "pool"
SKIPQK = False
import os as _os
KF = set(_os.environ.get("KF", "").split(","))
NCORES = int(_os.environ.get("KCORES", "8"))
STOP = 99
EPS = 1e-6


class V:
    __slots__ = ("ap", "keys")

    def __getitem__(self, idx):
        v = V.__new__(V)
        v.ap = self.ap[idx]
        v.keys = self.keys
        return v

    def rearrange(self, pat, **kw):
        v = V.__new__(V)
        v.ap = self.ap.rearrange(pat, **kw)
        v.keys = self.keys
        return v

    def bcast(self, shape):
        v = V.__new__(V)
        v.ap = self.ap.broadcast_to(shape)
        v.keys = self.keys
        return v


def mkV(ap, *keys):
    v = V.__new__(V)
    v.ap = ap
    v.keys = tuple(keys)
    return v


ENGS = ("pe", "act", "dve", "pool", "sp")


class Prog:
    def __init__(self, nc):
        self.nc = nc
        self.ops = []
        self.final_dma = []

    def _keys(self, vs):
        out = []
        for v in vs:
            if isinstance(v, V):
                for k in v.keys:
                    if k is not None:
                        out.append(k)
        return out

    def op(self, eng, fn, reads, writes, dma=False, semkey=None):
        self.ops.append(dict(eng=eng, fn=fn, reads=self._keys(reads), writes=self._keys(writes),
                             dma=dma, semkey=semkey))
        return len(self.ops) - 1

    def barrier(self):
        self.ops.append(dict(eng=None, barrier=True))

    def mm(self, out, lhsT, rhs, start=True, stop=True):
        return self.op("pe", lambda e: e.matmul(out.ap, lhsT.ap, rhs.ap, start=start, stop=stop),
                       [lhsT, rhs], [out])

    def act(self, out, in_, func, bias=None, scale=None, accum_out=None):
        kw = {}
        if bias is not None:
            kw["bias"] = bias.ap if isinstance(bias, V) else bias
        if scale is not None:
            kw["scale"] = scale.ap if isinstance(scale, V) else scale
        if accum_out is not None:
            kw["accum_out"] = accum_out.ap
        return self.op("act", lambda e: e.activation(out.ap, in_.ap, func, **kw),
                       [in_, bias, scale], [out, accum_out])

    def tt(self, out, in0, in1, op, eng="dve"):
        return self.op(eng, lambda e: e.tensor_tensor(out.ap, in0.ap, in1.ap, op), [in0, in1], [out])

    def ts(self, out, in0, s1, s2, op0, op1=None, eng="dve"):
        a1 = s1.ap if isinstance(s1, V) else s1
        a2 = s2.ap if isinstance(s2, V) else s2
        kw = {}
        if op1 is not None:
            kw["op1"] = op1
        return self.op(eng, lambda e: e.tensor_scalar(out.ap, in0.ap, a1, a2, op0, **kw),
                       [in0, s1, s2], [out])

    def stt(self, out, in0, scalar, in1, op0, op1, eng="dve"):
        a = scalar.ap if isinstance(scalar, V) else scalar
        return self.op(eng, lambda e: e.scalar_tensor_tensor(out.ap, in0.ap, a, in1.ap, op0, op1),
                       [in0, scalar, in1], [out])

    def copy(self, out, in_, eng="dve"):
        if eng == "act":
            return self.op(eng, lambda e: e.copy(out.ap, in_.ap), [in_], [out])
        return self.op(eng, lambda e: e.tensor_copy(out.ap, in_.ap), [in_], [out])

    def recip(self, out, in_):
        return self.op("dve", lambda e: e.reciprocal(out.ap, in_.ap), [in_], [out])

    def memset(self, out, val, eng="dve"):
        return self.op(eng, lambda e: e.memset(out.ap, val), [], [out])

    def dma(self, out, in_, semkey, eng="sp", final=False):
        i = self.op(eng, lambda e: e.dma_start(out=out.ap, in_=in_.ap), [in_], [out], dma=True, semkey=semkey)
        if final:
            self.final_dma.append(i)
        return i

    def build(self, stack):
        nc = self.nc
        ops = self.ops
        last_w, readers, last_dma_on_sem = {}, {}, {}
        deps = [None] * len(ops)
        eng_last = {}
        barrier_deps = None
        for i, o in enumerate(ops):
            if o.get("barrier"):
                lb = list(eng_last.values()) + list(last_dma_on_sem.values())
                barrier_deps = sorted(set(lb))
                last_w.clear()
                readers.clear()
                continue
            d = set()
            if barrier_deps:
                d.update(barrier_deps)
            for k in o["reads"]:
                if k in last_w:
                    d.add(last_w[k])
                if isinstance(k, tuple) and k[0] == "ps":
                    for r in readers.get(k, ()):
                        if ops[r]["eng"] != o["eng"]:
                            d.add(r)
            for k in o["writes"]:
                if k in last_w:
                    d.add(last_w[k])
                d.update(readers.get(k, ()))
            if o["dma"]:
                sk = o["semkey"]
                if sk in last_dma_on_sem:
                    d.add(last_dma_on_sem[sk])
                last_dma_on_sem[sk] = i
            d.discard(i)
            if o["eng"] == "pe" and not o["dma"]:
                d = {j for j in d if not (ops[j]["eng"] == "pe" and not ops[j]["dma"])}
            deps[i] = d
            for k in o["reads"]:
                readers.setdefault(k, []).append(i)
            for k in o["writes"]:
                last_w[k] = i
                readers[k] = []
            if not o["dma"]:
                eng_last[o["eng"]] = i
        need_inc = [False] * len(ops)
        for i, o in enumerate(ops):
            if o.get("barrier"):
                continue
            for j in deps[i]:
                need_inc[j] = True
        sem_eng = {e: stack.enter_context(nc.semaphore("s_" + e)) for e in ENGS}
        dma_keys = []
        seen = set()
        for o in ops:
            if o.get("barrier"):
                continue
            if o["dma"] and o["semkey"] not in seen:
                seen.add(o["semkey"])
                dma_keys.append(o["semkey"])
        sem_dma = {k: stack.enter_context(nc.semaphore("d_%d" % n)) for n, k in enumerate(dma_keys)}
        self.n_sems = len(sem_eng) + len(sem_dma)
        token = [None] * len(ops)
        cnt_eng = {e: 0 for e in ENGS}
        cnt_dma = {k: 0 for k in dma_keys}
        for i, o in enumerate(ops):
            if o.get("barrier"):
                continue
            if o["dma"]:
                cnt_dma[o["semkey"]] += 16
                token[i] = (("d", o["semkey"]), cnt_dma[o["semkey"]])
            elif need_inc[i]:
                cnt_eng[o["eng"]] += 1
                token[i] = (("e", o["eng"]), cnt_eng[o["eng"]])
        self.counts = dict(cnt_eng)

        def sem_of(t):
            return sem_dma[t[1]] if t[0] == "d" else sem_eng[t[1]]

        per_eng = {e: [] for e in ENGS}
        for i, o in enumerate(ops):
            if o.get("barrier"):
                continue
            per_eng[o["eng"]].append(i)
        self.n_per_eng = {e: len(v) for e, v in per_eng.items()}
        final_tokens = [token[i] for i in self.final_dma]

        def emit(ename, eh):
            waited = {}
            for i in per_eng[ename]:
                o = ops[i]
                need = {}
                for j in deps[i]:
                    sk, val = token[j]
                    if need.get(sk, 0) < val:
                        need[sk] = val
                for sk, val in need.items():
                    if waited.get(sk, 0) < val:
                        eh.wait_ge(sem_of(sk), val)
                        waited[sk] = val
                ins = o["fn"](eh)
                if o["dma"]:
                    ins.then_inc(sem_dma[o["semkey"]], 16)
                elif need_inc[i]:
                    ins.then_inc(sem_eng[ename], 1)
            if ename == "sp":
                for sk, val in final_tokens:
                    if waited.get(sk, 0) < val:
                        eh.wait_ge(sem_of(sk), val)
                        waited[sk] = val

        block = stack.enter_context(nc.Block())

        @block.tensor
        def _(e):
            emit("pe", e)

        @block.scalar
        def _(e):
            emit("act", e)

        @block.vector
        def _(e):
            emit("dve", e)

        @block.gpsimd
        def _(e):
            emit("pool", e)

        @block.sync
        def _(e):
            emit("sp", e)


class Arena:
    def __init__(self, ap, nwords):
        self.ap = ap
        self.n = nwords
        self.off = 0
        self.peak = 0

    def alloc(self, name, shape, dt, key=None):
        nelem = 1
        for s in shape[1:]:
            nelem *= s
        esz = 4 if dt == F32 else 2
        nwords = (nelem * esz + 3) // 4
        assert self.off + nwords <= self.n, ("SBUF arena overflow", name, self.off, nwords, self.n)
        a = self.ap[:, self.off:self.off + nwords]
        self.off += nwords
        self.peak = max(self.peak, self.off)
        if dt != F32:
            a = a.bitcast(dt)[:, 0:nelem]
        if len(shape) == 3:
            a = a.rearrange("p (a b) -> p a b", a=shape[1])
        elif len(shape) == 4:
            a = a.rearrange("p (a b c) -> p a b c", a=shape[1], b=shape[2])
        if shape[0] != 128:
            a = a[0:shape[0]]
        return mkV(a, key if key is not None else name)

    def mark(self):
        return self.off

    def reset(self, m):
        self.off = m


D = 1024
NKV = 4352
NKT = 34


def build_program():
    nc = bass.Bass("TRN2", target_bir_lowering=False)

    def din(name, shape, dt=F32):
        return nc.dram_tensor(name, list(shape), dt, kind="ExternalInput").ap()

    x_kv = din("x_kv", [NKV, D])
    ccol_d = din("ccol", [128, 16])
    ada_w_d = din("ada_w", [2, D, 6144])
    ada_b_d = din("ada_b", [1, 2 * 6144])
    ng_d = din("ng", [128, 32])
    w_in_d = din("w_in", [D, 4096])
    qkg_d = din("qkg", [128, 4])
    cos_d = din("cosT", [128, NKV])
    sin_d = din("sinT", [128, NKV])
    band_d = din("band", [128, 7 * 4 * 128])
    poolw_d = din("poolw", [64, 4 * 64])
    pscale_d = din("pscale", [64, 4])
    wo_a_d = din("wo_a", [768, D])
    wo_p_d = din("wo_p", [64, 4 * D])
    subg_d = din("subg", [128, 1])
    lam_d = din("lam", [1, 256])
    w1_d = din("ffn_w1", [2, D, 4096])
    w2_d = din("ffn_w2", [2, 4096, D])
    odin_d = din("od_w_in", [D, 4096])
    lng_d = din("sg_ln_g", [1, 2048])
    lnb_d = din("sg_ln_b", [1, 2048])
    sgw_d = din("sg_wT", [128, 8 * 128])
    sgb_d = din("sg_b", [1, 8 * 128])
    odout_d = din("od_w_out", [2048, D])
    ident_d = din("ident", [128, 128])
    bones_d = din("bones", [128, 128])
    out_d = nc.dram_tensor("out", [2048, D], F32, kind="ExternalOutput").ap()
    dbg_d = nc.dram_tensor("dbg", [4, 2048, D], F32, kind="ExternalOutput").ap() if DEBUG else None
    KTd = nc.dram_tensor("KTd", [6, 128, NKV], BF16, kind="Internal").ap()
    QTd = nc.dram_tensor("QTd", [6, 128, 2048], BF16, kind="Internal").ap()
    Vd = nc.dram_tensor("Vd", [NKV, 768], BF16, kind="Internal").ap()

    st = ExitStack()
    NW = 51 * 1024 + 512
    arena_t = st.enter_context(nc.sbuf_tensor("arena", [128, NW], F32))
    ps_t = st.enter_context(nc.psum_tensor("ps", [128, 8, 512], F32))
    A = Arena(arena_t[:, :], NW)
    P = Prog(nc)

    def bank(b):
        return mkV(ps_t[:, b, :], ("ps", b))

    def banks(b, n):
        return mkV(ps_t[:, b:b + n, :], *[("ps", b + i) for i in range(n)])

    def dr(ap, key=None):
        return mkV(ap, key)

    identb = A.alloc("identb", [128, 128], BF16)
    bonesb = A.alloc("bonesb", [128, 128], BF16)
    onesb = A.alloc("onesb", [128, 128], BF16)
    onesf = A.alloc("onesf", [128, 128], F32)
    ng = A.alloc("ng", [128, 32], F32)
    ccol = A.alloc("ccol", [128, 16], F32)
    scol = A.alloc("scol", [128, 16], F32)
    scolb = A.alloc("scolb", [128, 16], BF16)
    qkg = A.alloc("qkg", [128, 4], F32)
    subg = A.alloc("subg", [128, 1], F32)
    subg2 = A.alloc("subg2", [128, 1], F32)
    neglam = A.alloc("neglam", [128, 1], F32)
    lamrow = A.alloc("lamrow", [1, 256], F32)
    lamt = A.alloc("lamt", [1, 8], F32)
    pscale = A.alloc("pscale", [64, 4], F32)
    modcols = {}
    for l in range(2):
        for nm in ("G1", "SH1", "G2", "SH2"):
            modcols[(l, nm)] = A.alloc("mc_%d_%s" % (l, nm), [128, 8], F32)
    modcols[("c", "G1")] = A.alloc("mc_c_G1", [128, 8], F32)
    modcols[("c", "SH1")] = A.alloc("mc_c_SH1", [128, 8], F32)
    gate = {}
    for l in range(2):
        for nm in ("g1", "g2"):
            gate[(l, nm)] = A.alloc("gate_%d_%s" % (l, nm), [128, 1024], F32)
    tmpcol = A.alloc("tmpcol", [128, 8], F32)
    NST = 4
    st_ssq = [A.alloc("ssq%d" % i, [128, 1], F32) for i in range(NST)]
    st_std = [A.alloc("std%d" % i, [128, 1], F32) for i in range(NST)]
    st_rstd = [A.alloc("rstd%d" % i, [128, 1], F32) for i in range(NST)]
    junk = A.alloc("junk", [128, 1024], BF16)
    junk2 = A.alloc("junk2", [128, 512], BF16)
    xs_buf = [A.alloc("xs%d" % i, [128, 1024], BF16) for i in range(4)]
    X1W = 16 * 1024
    x1 = [mkV(arena_t[:, NW - X1W + j * 1024: NW - X1W + (j + 1) * 1024], "x1_%d" % j) for j in range(16)]
    g_end = A.mark()
    u_store = [A.alloc("u_%d" % j, [128, 256], F32) for j in range(18)]
    u_end = A.mark()

    P.dma(identb, dr(ident_d), "c_ident", eng="pool")
    P.dma(bonesb, dr(bones_d), "c_bones", eng="pool")
    P.dma(ng, dr(ng_d), "c_ng")
    P.dma(ccol, dr(ccol_d), "c_ccol")
    P.dma(qkg, dr(qkg_d), "c_qkg")
    P.dma(subg, dr(subg_d), "c_subg")
    P.dma(lamrow, dr(lam_d), "c_lam")
    P.dma(pscale, dr(pscale_d), "c_pscale")
    P.memset(onesb, 1.0)
    P.memset(onesf, 1.0)
    P.act(scol, ccol, AF.Silu)
    P.copy(scolb, scol)

    LAM_INIT = 0.8 - 0.6 * math.exp(-0.3 * 0)
    P.tt(lamrow[:, 0:64], lamrow[:, 0:64], lamrow[:, 64:128], ALU.mult)
    P.tt(lamrow[:, 128:192], lamrow[:, 128:192], lamrow[:, 192:256], ALU.mult)
    P.op("dve", lambda e: e.tensor_reduce(lamt[:, 0:1].ap, lamrow[:, 0:64].ap, AX.X, ALU.add), [lamrow], [lamt])
    P.op("dve", lambda e: e.tensor_reduce(lamt[:, 1:2].ap, lamrow[:, 128:192].ap, AX.X, ALU.add), [lamrow], [lamt])
    P.act(lamt[:, 2:4], lamt[:, 0:2], AF.Exp)
    P.tt(lamt[:, 4:5], lamt[:, 3:4], lamt[:, 2:3], ALU.subtract)
    P.ts(lamt[:, 5:6], lamt[:, 4:5], -LAM_INIT, None, ALU.add)
    P.mm(bank(0)[:, 0:1], onesf[0:1, :], lamt[:, 5:6])
    P.copy(neglam, bank(0)[:, 0:1])
    P.ts(subg2, subg, 1.0 - LAM_INIT, None, ALU.mult)

    def adaln(l):
        m = A.mark()
        modrow = A.alloc("modrow", [1, 6144], F32)
        ctxrow = A.alloc("ctxrow", [1, 2048], F32)
        adab = A.alloc("adab", [1, 6144], F32)
        awb = [A.alloc("awb%d" % i, [128, 8, 512], BF16) for i in range(3)]
        P.dma(adab, dr(ada_b_d[:, l * 6144:(l + 1) * 6144]), "adab")
        for nb in range(12):
            wb = awb[nb % 3]
            P.dma(wb, dr(ada_w_d[l].rearrange("(c p) n -> p c n", p=128)[:, :, nb * 512:(nb + 1) * 512]),
                  "awb%d" % (nb % 3), eng="pool")
            for k in range(8):
                P.mm(bank(0)[0:1, :], scolb[:, k:k + 1], wb[:, k, :], start=(k == 0), stop=(k == 7))
            P.tt(modrow[:, nb * 512:(nb + 1) * 512], bank(0)[0:1, :], adab[:, nb * 512:(nb + 1) * 512], ALU.add)
            if l == 0 and nb < 4:
                for k in range(8):
                    P.mm(bank(1)[0:1, :], scolb[:, 8 + k:9 + k], wb[:, k, :], start=(k == 0), stop=(k == 7))
                P.tt(ctxrow[:, nb * 512:(nb + 1) * 512], bank(1)[0:1, :], adab[:, nb * 512:(nb + 1) * 512], ALU.add)

        def cols(row, v, dst_sh, dst_g, gcol0):
            pb = bank(2)
            for k in range(8):
                P.mm(pb[:, k:k + 1], row[0:1, v * 1024 + k * 128: v * 1024 + (k + 1) * 128], onesf[0:1, 0:1])
                P.mm(pb[:, 8 + k:9 + k], row[0:1, (v + 1) * 1024 + k * 128:(v + 1) * 1024 + (k + 1) * 128],
                     onesf[0:1, 0:1])
            P.copy(dst_sh, pb[:, 0:8])
            P.ts(tmpcol, pb[:, 8:16], 1.0, None, ALU.add)
            P.tt(dst_g, tmpcol, ng[:, gcol0:gcol0 + 8], ALU.mult)

        cols(modrow, 0, modcols[(l, "SH1")], modcols[(l, "G1")], l * 16)
        cols(modrow, 3, modcols[(l, "SH2")], modcols[(l, "G2")], l * 16 + 8)
        if l == 0:
            cols(ctxrow, 0, modcols[("c", "SH1")], modcols[("c", "G1")], 0)
        for nm, v in (("g1", 2), ("g2", 5)):
            for half in range(2):
                P.mm(bank(3), onesf[0:1, :], modrow[0:1, v * 1024 + half * 512: v * 1024 + (half + 1) * 512])
                P.copy(gate[(l, nm)][:, half * 512:(half + 1) * 512], bank(3))
        P.barrier()
        A.reset(m)

    stat_i = [0]

    def norm_a(xt):
        i = stat_i[0] % NST
        stat_i[0] += 1
        ssq, std, rstd = st_ssq[i], st_std[i], st_rstd[i]
        xs = xs_buf[i]
        P.act(junk, xt, AF.Square, accum_out=ssq)
        P.act(std, ssq, AF.Sqrt, scale=1.0 / D, bias=EPS)
        P.recip(rstd, std)
        P.act(xs, xt, AF.Copy, scale=rstd)
        return xs

    def norm_b(xs, Gc, SHc, hT, col0, trb):
        for half in range(2):
            pb = bank(trb + half)
            for cc in range(4):
                c = half * 4 + cc
                P.mm(pb[:, cc * 128:(cc + 1) * 128], xs[:, c * 128:(c + 1) * 128], identb)
            for cc in range(4):
                c = half * 4 + cc
                P.ts(hT[:, c, col0:col0 + 128], pb[:, cc * 128:(cc + 1) * 128], Gc[:, c:c + 1], SHc[:, c:c + 1],
                     ALU.mult, ALU.add)

    class WStream:
        def __init__(self, nslots):
            self.slots = [A.alloc("wslot%d" % i, [128, 8, 512], BF16) for i in range(nslots)]
            self.n = nslots
            self.plan = []
            self.issued = 0
            self.ptr = 0
            self.g0 = 0

        def set_plan(self, plan):
            assert self.ptr == len(self.plan) and self.issued == len(self.plan)
            self.g0 += len(self.plan)
            self.plan = list(plan)
            self.issued = 0
            self.ptr = 0

        def get(self):
            upto = min(self.ptr + self.n, len(self.plan))
            while self.issued < upto:
                g = self.issued
                kc, src = self.plan[g]
                si = (self.g0 + g) % self.n
                P.dma(self.slots[si][:, 0:kc, :], dr(src), "wslot%d" % si, eng="pool")
                self.issued += 1
            g = self.ptr
            self.ptr += 1
            return self.slots[(self.g0 + g) % self.n]

    def wsrc(w_ap, row0, nk, col0):
        return w_ap.rearrange("(c p) n -> p c n", p=128)[:, row0:row0 + nk, col0:col0 + 512]

    def resid_add(tile_j, half, pbank, gb):
        resid_add_many([(tile_j, pbank)], half, gb)

    def resid_add_many(items, half, gb):
        if len(items) > len(rtmp):
            for c0 in range(0, len(items), len(rtmp)):
                resid_add_many(items[c0:c0 + len(rtmp)], half, gb)
            return
        tmps = []
        for n, (tile_j, pbank) in enumerate(items):
            tmp = rtmp[(rt_i[0]) % len(rtmp)]
            rt_i[0] += 1
            P.tt(tmp, pbank, gb[:, half * 512:(half + 1) * 512], ALU.mult)
            tmps.append(tmp)
        for (tile_j, pbank), tmp in zip(items, tmps):
            xv = x1[tile_j][:, half * 512:(half + 1) * 512]
            P.tt(xv, xv, tmp, ALU.add)

    rt_i = [0]

    def dump(idx):
        if DEBUG:
            for j in range(16):
                P.dma(dr(dbg_d[idx, j * 128:(j + 1) * 128, :]), x1[j], "dbg%d" % (j % 4), final=True)

    def finish():
        P.barrier()
        for j in range(16):
            P.dma(dr(out_d[j * 128:(j + 1) * 128, :]), x1[j], "out%d" % (j % 4), final=True)
        P.build(st)
        return nc, st, P, A

    adaln(0)
    adaln(1)
    if STOP == 0:
        return finish()

    A.reset(u_end)
    w_in = A.alloc("w_in", [128, 8, 4096], BF16)
    cosT = A.alloc("cosT", [128, NKV], F32)
    sinT = A.alloc("sinT", [128, NKV], F32)
    xt_buf = [A.alloc("xt%d" % i, [128, 1024], F32) for i in range(4)]
    hT_buf = [A.alloc("hT%d" % i, [128, 8, 512], BF16) for i in range(2)]
    sq_buf = [A.alloc("sq%d" % i, [128, 512], BF16) for i in range(2)]
    std_buf = [A.alloc("stdb%d" % i, [128, 512], F32) for i in range(2)]
    r_buf = [A.alloc("rb%d" % i, [128, 512], F32) for i in range(2)]
    t1_buf = [A.alloc("t1b%d" % i, [128, 512], F32) for i in range(2)]
    t2_buf = [A.alloc("t2b%d" % i, [128, 512], F32) for i in range(2)]
    kst_buf = [A.alloc("kst%d" % i, [128, 512], BF16) for i in range(3)]
    vst_buf = [A.alloc("vst%d" % i, [128, 768], BF16) for i in range(2)]

    for g in range(8):
        P.dma(w_in[:, :, g * 512:(g + 1) * 512], dr(wsrc(w_in_d, 0, 8, g * 512)), "w_in%d" % (g % 4), eng="pool")
    if "nocos" not in KF:
        P.dma(cosT, dr(cos_d), "cosT")
        P.dma(sinT, dr(sin_d), "sinT")

    unit_i = [0]
    pending = [None]
    xt_i = [0]

    def qk_unit(h, which, hT, tb, ntok):
        ui = unit_i[0]
        unit_i[0] += 1
        s = ui % 2
        pk, pks = bank(3 + 2 * s), bank(4 + 2 * s)
        cbase = 1024 + h * 512 + which * 256
        for c in range(8):
            P.mm(pk[:, 0:ntok], w_in[:, c, cbase:cbase + 128], hT[:, c, 0:ntok], start=(c == 0), stop=(c == 7))
        for c in range(8):
            P.mm(pks[:, 0:ntok], w_in[:, c, cbase + 128:cbase + 256], hT[:, c, 0:ntok], start=(c == 0), stop=(c == 7))
        sq, stdb, rb, t1, t2 = sq_buf[s], std_buf[s], r_buf[s], t1_buf[s], t2_buf[s]
        P.act(sq[:, 0:ntok], pk[:, 0:ntok], AF.Square)
        gcol, gscol = (qkg[:, 0:1], qkg[:, 1:2]) if which == 0 else (qkg[:, 2:3], qkg[:, 3:4])
        c0 = tb * 512
        P.stt(t1[:, 0:ntok], pk[:, 0:ntok], gcol, cosT[:, c0:c0 + ntok], ALU.mult, ALU.mult)
        P.stt(t2[:, 0:ntok], pks[:, 0:ntok], gscol, sinT[:, c0:c0 + ntok], ALU.mult, ALU.mult)
        P.tt(t1[:, 0:ntok], t1[:, 0:ntok], t2[:, 0:ntok], ALU.add)

        def epi():
            P.mm(bank(1)[:, 0:ntok], bonesb, sq[:, 0:ntok])
            P.act(stdb[:, 0:ntok], bank(1)[:, 0:ntok], AF.Sqrt, scale=1.0 / 64, bias=EPS)
            P.recip(rb[:, 0:ntok], stdb[:, 0:ntok])
            kst = kst_buf[ui % 3]
            P.tt(kst[:, 0:ntok], t1[:, 0:ntok], rb[:, 0:ntok], ALU.mult)
            if which == 0:
                P.dma(dr(QTd[h][:, c0:c0 + ntok], ("QTd", h)), kst[:, 0:ntok], "kst%d" % (ui % 3))
            else:
                P.dma(dr(KTd[h][:, c0:c0 + ntok], ("KTd", h)), kst[:, 0:ntok], "kst%d" % (ui % 3))

        if pending[0] is not None:
            pending[0]()
        pending[0] = epi

    def p1_blk(tb):
        ntile = 4 if tb < 8 else 2
        Gc, SHc = (modcols[(0, "G1")], modcols[(0, "SH1")]) if tb < 8 else (modcols[("c", "G1")], modcols[("c", "SH1")])
        return ntile, ntile * 128, hT_buf[tb % 2], Gc, SHc

    p1_xs = {}

    def p1_norm_a(tb, tiles=None):
        ntile, ntok, hT, Gc, SHc = p1_blk(tb)
        for j in (range(ntile) if tiles is None else [t for t in tiles if t < ntile]):
            kvt = tb * 4 + j
            xt = xt_buf[xt_i[0] % 4]
            P.dma(xt, dr(x_kv[kvt * 128:(kvt + 1) * 128, :]), "xt%d" % (xt_i[0] % 4))
            xt_i[0] += 1
            p1_xs[(tb, j)] = norm_a(xt)

    def p1_norm_b(tb):
        ntile, ntok, hT, Gc, SHc = p1_blk(tb)
        for j in range(ntile):
            norm_b(p1_xs[(tb, j)], Gc, SHc, hT, j * 128, 0)

    p1_norm_a(0)
    p1_norm_b(0)
    for tb in range(9):
        ntile, ntok, hT, Gc, SHc = p1_blk(tb)
        for j in range(ntile):
            kvt = tb * 4 + j
            vst = vst_buf[kvt % 2]
            pb, pb2 = bank(2), bank(7)
            for c in range(8):
                P.mm(pb, hT[:, c, j * 128:(j + 1) * 128], w_in[:, c, 0:512], start=(c == 0), stop=(c == 7))
            for c in range(8):
                P.mm(pb2, hT[:, c, j * 128:(j + 1) * 128], w_in[:, c, 512:1024], start=(c == 0), stop=(c == 7))
            uslot = kvt if kvt < 16 else (17 if kvt == 16 else (16 if kvt == 31 else None))
            if uslot is not None:
                P.copy(u_store[uslot], pb[:, 0:256])
            P.copy(vst[:, 0:256], pb[:, 256:512], eng="act")
            P.copy(vst[:, 256:768], pb2, eng="act")
            P.dma(dr(Vd[kvt * 128:(kvt + 1) * 128, :], ("Vd", kvt)), vst, "vst%d" % (kvt % 2))
        units = []
        for h in range(6):
            if tb < 4:
                units.append((h, 0))
            units.append((h, 1))
        for idx, (h, which) in enumerate(units):
            qk_unit(h, which, hT, tb, ntok)
            if idx < 4 and tb + 1 < 9:
                p1_norm_a(tb + 1, [idx])
            if idx == 4 and tb + 1 < 9:
                p1_norm_b(tb + 1)
    if pending[0] is not None:
        pending[0]()
    pending[0] = None
    P.barrier()
    if STOP == 1:
        return finish()

    A.reset(u_end)
    attnT = A.alloc("attnT", [128, 6, 2048], BF16)
    m2 = A.mark()
    kt_buf = [A.alloc("ktb%d" % i, [128, NKV], BF16) for i in range(2)]
    v_buf = [A.alloc("vb%d" % i, [128, NKT, 128], BF16) for i in range(2)]
    qt_buf = [A.alloc("qtb%d" % i, [128, 2048], BF16) for i in range(2)]
    e_buf = [A.alloc("eb%d" % i, [128, 2, 512], BF16) for i in range(4)]
    ep_buf = [A.alloc("ep%d" % i, [128, 2, 512], BF16) for i in range(2)]
    rz = [A.alloc("rz%d" % i, [128, 512], F32) for i in range(2)]
    a_t = [[A.alloc("at%d_%d" % (i, c), [128, 512], F32) for c in range(2)] for i in range(2)]
    asq = [A.alloc("asq%d" % i, [128, 512], BF16) for i in range(2)]
    astd = A.alloc("astd", [128, 512], F32)
    ar = A.alloc("ar", [128, 512], F32)

    def load_head(h):
        s = h % 2
        kvkeys = tuple(("Vd", t) for t in range(NKT))
        P.dma(kt_buf[s], dr(KTd[h], ("KTd", h)), "ktb%d" % s)
        for q4, (t0, t1) in enumerate(((0, 9), (9, 18), (18, 26), (26, 34))):
            P.dma(v_buf[s][:, t0:t1, :],
                  mkV(Vd.rearrange("(t p) e -> p t e", p=128)[:, t0:t1, h * 128:(h + 1) * 128], *kvkeys),
                  "vb%d_%d" % (s, q4))
        P.dma(qt_buf[s], dr(QTd[h], ("QTd", h)), "qtb%d" % s)

    load_head(0)
    ei = 0
    it = 0
    att_pending = [None]
    for h in range(6):
        if h + 1 < 6:
            load_head(h + 1)
        s = h % 2
        ktb, vb, qtb = kt_buf[s], v_buf[s], qt_buf[s]
        for qb in range(4):
            qs = slice(qb * 512, (qb + 1) * 512)
            ip = it % 2
            it += 1
            at0, at1 = a_t[ip]

            def s_mm(kt):
                sset = kt % 2
                ks = slice(kt * 128, (kt + 1) * 128)
                P.mm(bank(2 * sset), ktb[0:64, ks], qtb[0:64, qs])
                P.mm(bank(2 * sset + 1), ktb[64:128, ks], qtb[64:128, qs])

            s_mm(0)
            s_mm(1)
            for kt in range(NKT):
                sset = kt % 2
                E = e_buf[ei % 4]
                ei += 1
                P.act(E, banks(2 * sset, 2), AF.Exp, scale=0.125, bias=-4.0)
                if kt + 2 < NKT:
                    s_mm(kt + 2)
                first, last = (kt == 0), (kt == NKT - 1)
                P.mm(bank(4), vb[:, kt, :], E[:, 0, :], start=first, stop=last)
                P.mm(bank(5), vb[:, kt, :], E[:, 1, :], start=first, stop=last)
                if kt % 2 == 0:
                    Eprev = E
                else:
                    Ep = ep_buf[(kt // 2) % 2]
                    P.tt(Ep, Eprev, E, ALU.add)
                    P.mm(bank(6), onesb, Ep[:, 0, :], start=(kt == 1), stop=last)
                    P.mm(bank(7), onesb, Ep[:, 1, :], start=(kt == 1), stop=last)
                if kt == 0 and att_pending[0] is not None:
                    att_pending[0]()
                    att_pending[0] = None
            P.copy(at0, bank(4))
            P.copy(at1, bank(5))
            P.recip(rz[0], bank(6))
            P.recip(rz[1], bank(7))
            P.tt(at0, at0, rz[0], ALU.mult)
            P.tt(at1, at1, rz[1], ALU.mult)
            P.stt(at0, at1, neglam, at0, ALU.mult, ALU.add)
            P.tt(asq[ip], at0, at0, ALU.mult)

            def epi(h=h, qs=qs, ip=ip, at0=at0):
                P.mm(bank(6), onesb, asq[ip])
                P.act(astd, bank(6), AF.Ln, scale=1.0 / 128, bias=EPS)
                P.act(ar, astd, AF.Exp, scale=-0.5)
                P.stt(attnT[:, h, qs], at0, subg2, ar, ALU.mult, ALU.mult)

            att_pending[0] = epi
    att_pending[0]()
    P.barrier()
    if STOP == 2:
        return finish()

    A.reset(m2)
    A.n = NW - X1W
    band = A.alloc("band", [128, 7, 4, 128], F32)
    poolw = A.alloc("poolw", [64, 4, 64], BF16)
    wo_a = A.alloc("wo_a", [128, 6, 1024], BF16)
    wo_p = A.alloc("wo_p", [64, 4, 1024], BF16)
    pooledT = [A.alloc("pooledT%d" % i, [64, 4, 128], BF16) for i in range(2)]
    ypoolT = [A.alloc("ypoolT%d" % i, [64, 4, 128], BF16) for i in range(2)]
    rtmp = [A.alloc("rtmp%d" % i, [128, 512], F32) for i in range(4)]
    P.dma(band, dr(band_d.rearrange("p (k g t) -> p k g t", k=7, g=4)), "band")
    P.dma(poolw, dr(poolw_d.rearrange("p (g d) -> p g d", g=4)), "poolw", eng="pool")
    P.dma(wo_a, dr(wo_a_d.rearrange("(h p) n -> p h n", p=128)), "wo_a", eng="pool")
    P.dma(wo_p, dr(wo_p_d.rearrange("p (g n) -> p g n", g=4)), "wo_p", eng="pool")
    for j in range(16):
        P.dma(x1[j], dr(x_kv[j * 128:(j + 1) * 128, :]), "x1_%d" % (j % 4))
    for j in range(16):
        pT, yT = pooledT[j % 2], ypoolT[j % 2]
        up = u_store[16] if j == 0 else u_store[j - 1]
        un = u_store[17] if j == 15 else u_store[j + 1]
        kp, km, kn = (0, 1, 4) if j == 0 else ((2, 5, 6) if j == 15 else (2, 3, 4))
        pb = bank(0)
        for g in range(4):
            o = pb[0:64, g * 128:(g + 1) * 128]
            gs = slice(g * 64, (g + 1) * 64)
            P.mm(o, up[:, gs], band[:, kp, g, :], start=True, stop=False)
            P.mm(o, u_store[j][:, gs], band[:, km, g, :], start=False, stop=False)
            P.mm(o, un[:, gs], band[:, kn, g, :], start=False, stop=True)
        P.copy(pT, pb[0:64, :].rearrange("p (g t) -> p g t", g=4))
        pb2 = bank(1)
        for g in range(4):
            P.mm(pb2[0:64, g * 128:(g + 1) * 128], poolw[:, g, :], pT[:, g, :])
        for g in range(4):
            P.ts(yT[:, g, :], pb2[0:64, g * 128:(g + 1) * 128], pscale[:, g:g + 1], None, ALU.mult)
        for half in range(2):
            pby = bank(2 + ((2 * j + half) % 4))
            hs = slice(half * 512, (half + 1) * 512)
            for h in range(6):
                P.mm(pby, attnT[:, h, j * 128:(j + 1) * 128], wo_a[:, h, hs], start=(h == 0), stop=False)
            for g in range(4):
                P.mm(pby, yT[:, g, :], wo_p[:, g, hs], start=False, stop=(g == 3))
            resid_add(j, half, pby, gate[(0, "g1")])
    P.barrier()
    dump(0)
    if STOP == 3:
        return finish()

    def ffn(l, ws, hT_b, hid, relu_t):
        G2, SH2, gb = modcols[(l, "G2")], modcols[(l, "SH2")], gate[(l, "g2")]
        plan = []
        for tb in range(4):
            for hg in range(8):
                plan.append((8, wsrc(w1_d[l], 0, 8, hg * 512)))
            for half in range(2):
                for sg in range(4):
                    plan.append((8, wsrc(w2_d[l], sg * 8, 8, half * 512)))
        ws.set_plan(plan)
        fxs = {}

        def f_norm_a(tb, tiles=range(4)):
            for j in tiles:
                fxs[(tb, j)] = norm_a(x1[tb * 4 + j])

        def f_norm_b(tb):
            for j in range(4):
                norm_b(fxs[(tb, j)], G2, SH2, hT_b[tb % 2], j * 128, 0)

        f_norm_a(0)
        f_norm_b(0)
        for tb in range(4):
            hT = hT_b[tb % 2]
            for hg in range(8):
                wg = ws.get()
                for hc in range(4):
                    pb = bank(2 + (hg * 4 + hc) % 2)
                    for c in range(8):
                        P.mm(pb, wg[:, c, hc * 128:(hc + 1) * 128], hT[:, c, :], start=(c == 0), stop=(c == 7))
                    rt = relu_t[(hg * 4 + hc) % 2]
                    P.act(rt, pb, AF.Relu)
                    P.tt(hid[:, hg * 4 + hc, :], rt, rt, ALU.mult)
                if hg < 4 and tb + 1 < 4:
                    f_norm_a(tb + 1, [hg])
                if hg == 5 and tb + 1 < 4:
                    f_norm_b(tb + 1)
            for half in range(2):
                for sg in range(4):
                    wg = ws.get()
                    for j in range(4):
                        for k in range(8):
                            P.mm(bank(4 + j), hid[:, sg * 8 + k, j * 128:(j + 1) * 128], wg[:, k, :],
                                 start=(sg == 0 and k == 0), stop=(sg == 3 and k == 7))
                resid_add_many([(tb * 4 + j, bank(4 + j)) for j in range(4)], half, gb)

    A.reset(g_end)
    rtmp = [A.alloc("rtmp%d" % i, [128, 512], F32) for i in range(4)]
    hT_b = [A.alloc("hTf%d" % i, [128, 8, 512], BF16) for i in range(2)]
    hid = A.alloc("hid", [128, 32, 512], BF16)
    relu_t = [A.alloc("relut%d" % i, [128, 512], F32) for i in range(2)]
    ws = WStream(6)
    ffn(0, ws, hT_b, hid, relu_t)
    P.barrier()
    dump(1)
    if STOP == 4:
        return finish()

    A.reset(g_end)
    rtmp = [A.alloc("rtmp%d" % i, [128, 512], F32) for i in range(2)]
    hT_b = [A.alloc("hTf%d" % i, [128, 8, 512], BF16) for i in range(2)]
    ms = A.mark()
    uT = A.alloc("uT", [128, 16, 512], BF16)
    v_bf = [A.alloc("v_bf%d" % j, [128, 2048], BF16) for j in range(4)]
    vn = [A.alloc("vn%d" % i, [128, 2048], BF16) for i in range(2)]
    gT = uT
    gelu_t = [A.alloc("gelut%d" % i, [128, 512], F32) for i in range(2)]
    lng = A.alloc("lng", [128, 2048], BF16)
    lnb = A.alloc("lnb", [128, 2048], BF16)
    sgw = A.alloc("sgw", [128, 8, 128], BF16)
    sgb = A.alloc("sgb", [1, 8, 128], F32)
    lst = A.alloc("lst", [128, 4, 8], F32)
    lst2 = A.alloc("lst2", [128, 4, 8], F32)
    ws = WStream(4)
    P.dma(lng, dr(lng_d.broadcast_to([128, 2048])), "lng", eng="pool")
    P.dma(lnb, dr(lnb_d.broadcast_to([128, 2048])), "lnb", eng="pool")
    P.dma(sgw, dr(sgw_d.rearrange("p (g m) -> p g m", g=8)), "sgw", eng="pool")
    P.dma(sgb, dr(sgb_d.rearrange("p (g m) -> p g m", g=8)), "sgb")
    G1, SH1, gb1 = modcols[(1, "G1")], modcols[(1, "SH1")], gate[(1, "g1")]
    plan = []
    for tb in range(4):
        for ug in range(4):
            plan.append((8, wsrc(odin_d, 0, 8, ug * 512)))
        for vg in range(4):
            plan.append((8, wsrc(odin_d, 0, 8, 2048 + vg * 512)))
        for half in range(2):
            for sg in range(2):
                plan.append((8, wsrc(odout_d, sg * 8, 8, half * 512)))
    ws.set_plan(plan)
    gi = 0
    sxs = {}

    def s_norm_a(tb, tiles=range(4)):
        for j in tiles:
            sxs[(tb, j)] = norm_a(x1[tb * 4 + j])

    def s_norm_b(tb):
        for j in range(4):
            norm_b(sxs[(tb, j)], G1, SH1, hT_b[tb % 2], j * 128, 0)

    s_norm_a(0)
    s_norm_b(0)
    for tb in range(4):
        hT = hT_b[tb % 2]
        for ug in range(4):
            wg = ws.get()
            if tb + 1 < 4:
                s_norm_a(tb + 1, [ug])
            for hc in range(4):
                pb = bank(2 + (ug * 4 + hc) % 2)
                for c in range(8):
                    P.mm(pb, wg[:, c, hc * 128:(hc + 1) * 128], hT[:, c, :], start=(c == 0), stop=(c == 7))
                P.act(uT[:, ug * 4 + hc, :], pb, AF.Gelu)
        for vg in range(4):
            wg = ws.get()
            if vg == 3 and tb + 1 < 4:
                s_norm_b(tb + 1)
            for j in range(4):
                pb = bank(2 + (vg * 4 + j) % 2)
                for c in range(8):
                    P.mm(pb, hT[:, c, j * 128:(j + 1) * 128], wg[:, c, :], start=(c == 0), stop=(c == 7))
                gt = gelu_t[gi % 2]
                gi += 1
                P.act(gt, pb, AF.Gelu, accum_out=lst[:, j, vg:vg + 1])
                P.act(junk2, gt, AF.Square, accum_out=lst[:, j, 4 + vg:5 + vg])
                P.copy(v_bf[j][:, vg * 512:(vg + 1) * 512], gt)
        for j in range(4):
            P.op("dve", lambda e, j=j: e.tensor_reduce(lst2[:, j, 0:1].ap, lst[:, j, 0:4].ap, AX.X, ALU.add), [lst], [lst2])
            P.op("dve", lambda e, j=j: e.tensor_reduce(lst2[:, j, 1:2].ap, lst[:, j, 4:8].ap, AX.X, ALU.add), [lst], [lst2])
            P.ts(lst2[:, j, 2:3], lst2[:, j, 0:1], 1.0 / 2048, None, ALU.mult)
            P.tt(lst2[:, j, 3:4], lst2[:, j, 2:3], lst2[:, j, 2:3], ALU.mult)
            P.stt(lst2[:, j, 4:5], lst2[:, j, 1:2], 1.0 / 2048, lst2[:, j, 3:4], ALU.mult, ALU.subtract)
            P.act(lst2[:, j, 5:6], lst2[:, j, 4:5], AF.Sqrt, bias=EPS)
            P.recip(lst2[:, j, 6:7], lst2[:, j, 5:6])
            vv = vn[j % 2]
            P.ts(vv, v_bf[j], lst2[:, j, 2:3], lst2[:, j, 6:7], ALU.subtract, ALU.mult)
            P.tt(vv, vv, lng, ALU.mult)
            P.tt(vv, vv, lnb, ALU.add, eng=POOL)
            for q4 in range(4):
                pb = bank(4 + q4)
                for cc4 in range(4):
                    cc = q4 * 4 + cc4
                    g = cc // 2
                    o = pb[:, cc4 * 128:(cc4 + 1) * 128]
                    P.mm(o, vv[:, cc * 128:(cc + 1) * 128], sgw[:, g, :], start=True, stop=False)
                    P.mm(o, onesf[0:1, :], sgb[:, g, :], start=False, stop=True)
                P.tt(gT[:, q4 * 4:(q4 + 1) * 4, j * 128:(j + 1) * 128],
                     uT[:, q4 * 4:(q4 + 1) * 4, j * 128:(j + 1) * 128],
                     pb.rearrange("p (c m) -> p c m", c=4), ALU.mult)
        for half in range(2):
            for sg in range(2):
                wg = ws.get()
                for j in range(4):
                    for k in range(8):
                        P.mm(bank(j), gT[:, sg * 8 + k, j * 128:(j + 1) * 128], wg[:, k, :],
                             start=(sg == 0 and k == 0), stop=(sg == 1 and k == 7))
            resid_add_many([(tb * 4 + j, bank(j)) for j in range(4)], half, gb1)
    P.barrier()
    dump(2)
    if STOP == 5:
        return finish()

    A.reset(ms)
    rtmp = rtmp + [A.alloc("rtmp%d" % i, [128, 512], F32) for i in (2, 3)]
    hid = A.alloc("hid", [128, 32, 512], BF16)
    relu_t = [A.alloc("relut%d" % i, [128, 512], F32) for i in range(2)]
    ws = WStream(6)
    ffn(1, ws, hT_b, hid, relu_t)
    return finish()


_CACHE = {}


def _rope_tables():
    t = np.arange(4096)
    rows = (t // 64).astype(np.float32)
    cols = (t % 64).astype(np.float32)
    inv = (np.float32(10000.0) ** (-np.arange(16, dtype=np.float32) / np.float32(16))).astype(np.float32)
    ang = np.concatenate([rows[:, None] * inv, cols[:, None] * inv], axis=-1).astype(np.float32)
    return np.cos(ang).astype(np.float32), np.sin(ang).astype(np.float32)


def _band(w, n, T0, kind_rel):
    M = np.zeros((128, 128), np.float32)
    for i in range(128):
        t = T0 + i
        lo = min(max(t - w // 2, 0), n)
        hi = min(max(t + w // 2, 0), n)
        cnt = float(hi - lo)
        for tp in range(lo, hi):
            ip = tp - (T0 + kind_rel * 128)
            if 0 <= ip < 128:
                M[ip, i] += np.float32(1.0) / np.float32(cnt)
        if kind_rel == 0:
            M[i, i] -= 1.0
    return M


def _bands(hf):
    wins = (2, 4, 8, 16)
    n = 4096
    out = np.zeros((128, 7, 4, 128), np.float32)
    Tmid = 1024
    for g, w in enumerate(wins):
        prev = _band(w, n, Tmid, -1)
        mid = _band(w, n, Tmid, 0)
        nxt = _band(w, n, Tmid, 1)
        if hf == 0:
            k0, k1 = np.zeros_like(prev), _band(w, n, 0, 0)
            k5, k6 = mid, nxt
        else:
            k0, k1 = prev, mid
            k5, k6 = _band(w, n, n - 128, 0), np.zeros_like(nxt)
        for k, m in enumerate((k0, k1, prev, mid, nxt, k5, k6)):
            out[:, k, g, :] = m
    return out.reshape(128, 7 * 4 * 128)


def _col(v):
    return np.ascontiguousarray(np.asarray(v, np.float32).reshape(-1, 128).T)


def kernel(x, c, ctx, c_ctx, ada_w, ada_b, norm_mix_g, norm_ffn_g, ffn_w1, ffn_w2,
           ev_w_in, ev_w_out, pool_w, pool_scale, q_norm_g, k_norm_g,
           lam_q1, lam_k1, lam_q2, lam_k2, sub_norm_g,
           od_w_in, sg_ln_g, sg_ln_b, sg_w, sg_b, od_w_out):
    f = lambda a: np.ascontiguousarray(np.asarray(a, dtype=np.float32))
    x, c, ctx, c_ctx = f(x), f(c), f(ctx), f(c_ctx)
    if "nc" not in _CACHE:
        _CACHE["nc"] = build_program()
    nc = _CACHE["nc"][0]

    w_in0 = f(ev_w_in)[0]
    swap = np.arange(128) ^ 1
    cols = [w_in0[:, 0:256], w_in0[:, 1792:2560]]
    for h in range(6):
        q = w_in0[:, 256 + h * 128:256 + (h + 1) * 128]
        k = w_in0[:, 1024 + h * 128:1024 + (h + 1) * 128]
        cols += [q, q[:, swap], k, k[:, swap]]
    w_in_r = np.ascontiguousarray(np.concatenate(cols, axis=1))
    gq, gk = f(q_norm_g)[0], f(k_norm_g)[0]
    d = np.arange(128) % 64
    qkg = np.stack([gq[d], gq[d ^ 1], gk[d], gk[d ^ 1]], axis=1).astype(np.float32)
    cos, sin = _rope_tables()
    pair = d // 2
    sign = np.where(d % 2 == 0, -1.0, 1.0).astype(np.float32)
    cos_full = cos[:, pair].T
    sin_full = (sin[:, pair] * sign[None, :]).T
    ng = np.concatenate([_col(f(norm_mix_g)[0]), _col(f(norm_ffn_g)[0]),
                         _col(f(norm_mix_g)[1]), _col(f(norm_ffn_g)[1])], axis=1)
    w_out0 = f(ev_w_out)[0]
    wo_p = np.ascontiguousarray(w_out0[0:256].reshape(4, 64, 1024).transpose(1, 0, 2).reshape(64, 4096))
    wo_a = np.ascontiguousarray(w_out0[256:1024])
    poolw = np.ascontiguousarray(f(pool_w)[0].transpose(1, 0, 2).reshape(64, 256))
    pscale = np.ascontiguousarray(f(pool_scale)[0].reshape(4, 64).T)
    lam = np.concatenate([f(lam_q1)[0], f(lam_k1)[0], f(lam_q2)[0], f(lam_k2)[0]])[None, :]
    sgwT = np.ascontiguousarray(f(sg_w)[0].transpose(2, 0, 1).reshape(128, 8 * 128))
    sgb = f(sg_b)[0].reshape(1, 8 * 128)
    ident = np.eye(128, dtype=np.float32)
    bones = np.zeros((128, 128), np.float32)
    bones[0:64, 0:64] = 1.0
    bones[64:128, 64:128] = 1.0
    shared = {
        "ada_w": f(ada_w), "ada_b": f(ada_b).reshape(1, -1), "ng": np.ascontiguousarray(ng),
        "w_in": w_in_r, "qkg": qkg, "poolw": poolw, "pscale": pscale, "wo_a": wo_a, "wo_p": wo_p,
        "subg": f(sub_norm_g)[0].reshape(128, 1), "lam": np.ascontiguousarray(lam),
        "ffn_w1": f(ffn_w1), "ffn_w2": f(ffn_w2), "od_w_in": f(od_w_in)[0],
        "sg_ln_g": f(sg_ln_g)[0].reshape(1, -1), "sg_ln_b": f(sg_ln_b)[0].reshape(1, -1),
        "sg_wT": sgwT, "sg_b": sgb, "od_w_out": f(od_w_out)[0], "ident": ident, "bones": bones,
    }
    in_maps = []
    for core in range(8):
        b, hf = core // 2, core % 2
        own = slice(hf * 2048, (hf + 1) * 2048)
        oth = slice((1 - hf) * 2048, (2 - hf) * 2048)
        x_kv = np.concatenate([x[b, own], x[b, oth], ctx[b]], axis=0)
        cosT = np.concatenate([cos_full[:, own], cos_full[:, oth], np.ones((128, 256), np.float32)], axis=1)
        sinT = np.concatenate([sin_full[:, own], sin_full[:, oth], np.zeros((128, 256), np.float32)], axis=1)
        m = dict(shared)
        m["x_kv"] = np.ascontiguousarray(x_kv)
        m["ccol"] = np.ascontiguousarray(np.concatenate([_col(c[b]), _col(c_ctx)], axis=1))
        m["cosT"] = np.ascontiguousarray(cosT)
        m["sinT"] = np.ascontiguousarray(sinT)
        m["band"] = _bands(hf)
        in_maps.append(m)
    import time as _time
    _t0 = _time.time()
    res = run_bass_kernel_spmd(nc, in_maps[:NCORES], core_ids=list(range(NCORES)), **({"trace": True} if _os.environ.get("KTRACE") else {}))
    if _os.environ.get("KTRACE"):
        print("spmd wall", _time.time() - _t0, "exec_time_ns", res.exec_time_ns, flush=True)
    out = np.zeros((4, 4096, 1024), np.float32)
    for core in range(NCORES):
        b, hf = core // 2, core % 2
        out[b, hf * 2048:(hf + 1) * 2048] = res.results[core]["out"]
    if DEBUG:
        _CACHE["dbg"] = [res.results[core]["dbg"] for core in range(NCORES)]
    return out
```

```python
import math
import numpy as np
from contextlib import ExitStack
import concourse.bass as bass
import concourse.mybir as mybir
from concourse.bass_utils import run_bass_kernel_spmd

F32 = mybir.dt.float32
BF16 = mybir.dt.bfloat16
AF = mybir.ActivationFunctionType
ALU = mybir.AluOpType
AX = mybir.AxisListType

DEBUG = False
POOL = "pool"
SKIPQK = False
import os as _os
KF = set(_os.environ.get("KF", "").split(","))
NCORES = int(_os.environ.get("KCORES", "8"))
STOP = 99
EPS = 1e-6


class V:
    __slots__ = ("ap", "keys")

    def __getitem__(self, idx):
        v = V.__new__(V)
        v.ap = self.ap[idx]
        v.keys = self.keys
        return v

    def rearrange(self, pat, **kw):
        v = V.__new__(V)
        v.ap = self.ap.rearrange(pat, **kw)
        v.keys = self.keys
        return v

    def bcast(self, shape):
        v = V.__new__(V)
        v.ap = self.ap.broadcast_to(shape)
        v.keys = self.keys
        return v


def mkV(ap, *keys):
    v = V.__new__(V)
    v.ap = ap
    v.keys = tuple(keys)
    return v


ENGS = ("pe", "act", "dve", "pool", "sp")


class Prog:
    def __init__(self, nc):
        self.nc = nc
        self.ops = []
        self.final_dma = []

    def _keys(self, vs):
        out = []
        for v in vs:
            if isinstance(v, V):
                for k in v.keys:
                    if k is not None:
                        out.append(k)
        return out

    def op(self, eng, fn, reads, writes, dma=False, semkey=None):
        self.ops.append(dict(eng=eng, fn=fn, reads=self._keys(reads), writes=self._keys(writes),
                             dma=dma, semkey=semkey))
        return len(self.ops) - 1

    def barrier(self):
        self.ops.append(dict(eng=None, barrier=True))

    def mm(self, out, lhsT, rhs, start=True, stop=True):
        return self.op("pe", lambda e: e.matmul(out.ap, lhsT.ap, rhs.ap, start=start, stop=stop),
                       [lhsT, rhs], [out])

    def act(self, out, in_, func, bias=None, scale=None, accum_out=None):
        kw = {}
        if bias is not None:
            kw["bias"] = bias.ap if isinstance(bias, V) else bias
        if scale is not None:
            kw["scale"] = scale.ap if isinstance(scale, V) else scale
        if accum_out is not None:
            kw["accum_out"] = accum_out.ap
        return self.op("act", lambda e: e.activation(out.ap, in_.ap, func, **kw),
                       [in_, bias, scale], [out, accum_out])

    def tt(self, out, in0, in1, op, eng="dve"):
        return self.op(eng, lambda e: e.tensor_tensor(out.ap, in0.ap, in1.ap, op), [in0, in1], [out])

    def ts(self, out, in0, s1, s2, op0, op1=None, eng="dve"):
        a1 = s1.ap if isinstance(s1, V) else s1
        a2 = s2.ap if isinstance(s2, V) else s2
        kw = {}
        if op1 is not None:
            kw["op1"] = op1
        return self.op(eng, lambda e: e.tensor_scalar(out.ap, in0.ap, a1, a2, op0, **kw),
                       [in0, s1, s2], [out])

    def stt(self, out, in0, scalar, in1, op0, op1, eng="dve"):
        a = scalar.ap if isinstance(scalar, V) else scalar
        return self.op(eng, lambda e: e.scalar_tensor_tensor(out.ap, in0.ap, a, in1.ap, op0, op1),
                       [in0, scalar, in1], [out])

    def copy(self, out, in_, eng="dve"):
        if eng == "act":
            return self.op(eng, lambda e: e.copy(out.ap, in_.ap), [in_], [out])
        return self.op(eng, lambda e: e.tensor_copy(out.ap, in_.ap), [in_], [out])

    def recip(self, out, in_):
        return self.op("dve", lambda e: e.reciprocal(out.ap, in_.ap), [in_], [out])

    def memset(self, out, val, eng="dve"):
        return self.op(eng, lambda e: e.memset(out.ap, val), [], [out])

    def dma(self, out, in_, semkey, eng="sp", final=False):
        i = self.op(eng, lambda e: e.dma_start(out=out.ap, in_=in_.ap), [in_], [out], dma=True, semkey=semkey)
        if final:
            self.final_dma.append(i)
        return i

    def build(self, stack):
        nc = self.nc
        ops = self.ops
        last_w, readers, last_dma_on_sem = {}, {}, {}
        deps = [None] * len(ops)
        eng_last = {}
        barrier_deps = None
        for i, o in enumerate(ops):
            if o.get("barrier"):
                lb = list(eng_last.values()) + list(last_dma_on_sem.values())
                barrier_deps = sorted(set(lb))
                last_w.clear()
                readers.clear()
                continue
            d = set()
            if barrier_deps:
                d.update(barrier_deps)
            for k in o["reads"]:
                if k in last_w:
                    d.add(last_w[k])
                if isinstance(k, tuple) and k[0] == "ps":
                    for r in readers.get(k, ()):
                        if ops[r]["eng"] != o["eng"]:
                            d.add(r)
            for k in o["writes"]:
                if k in last_w:
                    d.add(last_w[k])
                d.update(readers.get(k, ()))
            if o["dma"]:
                sk = o["semkey"]
                if sk in last_dma_on_sem:
                    d.add(last_dma_on_sem[sk])
                last_dma_on_sem[sk] = i
            d.discard(i)
            if o["eng"] == "pe" and not o["dma"]:
                d = {j for j in d if not (ops[j]["eng"] == "pe" and not ops[j]["dma"])}
            deps[i] = d
            for k in o["reads"]:
                readers.setdefault(k, []).append(i)
            for k in o["writes"]:
                last_w[k] = i
                readers[k] = []
            if not o["dma"]:
                eng_last[o["eng"]] = i
        need_inc = [False] * len(ops)
        for i, o in enumerate(ops):
            if o.get("barrier"):
                continue
            for j in deps[i]:
                need_inc[j] = True
        sem_eng = {e: stack.enter_context(nc.semaphore("s_" + e)) for e in ENGS}
        dma_keys = []
        seen = set()
        for o in ops:
            if o.get("barrier"):
                continue
            if o["dma"] and o["semkey"] not in seen:
                seen.add(o["semkey"])
                dma_keys.append(o["semkey"])
        sem_dma = {k: stack.enter_context(nc.semaphore("d_%d" % n)) for n, k in enumerate(dma_keys)}
        self.n_sems = len(sem_eng) + len(sem_dma)
        token = [None] * len(ops)
        cnt_eng = {e: 0 for e in ENGS}
        cnt_dma = {k: 0 for k in dma_keys}
        for i, o in enumerate(ops):
            if o.get("barrier"):
                continue
            if o["dma"]:
                cnt_dma[o["semkey"]] += 16
                token[i] = (("d", o["semkey"]), cnt_dma[o["semkey"]])
            elif need_inc[i]:
                cnt_eng[o["eng"]] += 1
                token[i] = (("e", o["eng"]), cnt_eng[o["eng"]])
        self.counts = dict(cnt_eng)

        def sem_of(t):
            return sem_dma[t[1]] if t[0] == "d" else sem_eng[t[1]]

        per_eng = {e: [] for e in ENGS}
        for i, o in enumerate(ops):
            if o.get("barrier"):
                continue
            per_eng[o["eng"]].append(i)
        self.n_per_eng = {e: len(v) for e, v in per_eng.items()}
        final_tokens = [token[i] for i in self.final_dma]

        def emit(ename, eh):
            waited = {}
            for i in per_eng[ename]:
                o = ops[i]
                need = {}
                for j in deps[i]:
                    sk, val = token[j]
                    if need.get(sk, 0) < val:
                        need[sk] = val
                for sk, val in need.items():
                    if waited.get(sk, 0) < val:
                        eh.wait_ge(sem_of(sk), val)
                        waited[sk] = val
                ins = o["fn"](eh)
                if o["dma"]:
                    ins.then_inc(sem_dma[o["semkey"]], 16)
                elif need_inc[i]:
                    ins.then_inc(sem_eng[ename], 1)
            if ename == "sp":
                for sk, val in final_tokens:
                    if waited.get(sk, 0) < val:
                        eh.wait_ge(sem_of(sk), val)
                        waited[sk] = val

        block = stack.enter_context(nc.Block())

        @block.tensor
        def _(e):
            emit("pe", e)

        @block.scalar
        def _(e):
            emit("act", e)

        @block.vector
        def _(e):
            emit("dve", e)

        @block.gpsimd
        def _(e):
            emit("pool", e)

        @block.sync
        def _(e):
            emit("sp", e)


class Arena:
    def __init__(self, ap, nwords):
        self.ap = ap
        self.n = nwords
        self.off = 0
        self.peak = 0

    def alloc(self, name, shape, dt, key=None):
        nelem = 1
        for s in shape[1:]:
            nelem *= s
        esz = 4 if dt == F32 else 2
        nwords = (nelem * esz + 3) // 4
        assert self.off + nwords <= self.n, ("SBUF arena overflow", name, self.off, nwords, self.n)
        a = self.ap[:, self.off:self.off + nwords]
        self.off += nwords
        self.peak = max(self.peak, self.off)
        if dt != F32:
            a = a.bitcast(dt)[:, 0:nelem]
        if len(shape) == 3:
            a = a.rearrange("p (a b) -> p a b", a=shape[1])
        elif len(shape) == 4:
            a = a.rearrange("p (a b c) -> p a b c", a=shape[1], b=shape[2])
        if shape[0] != 128:
            a = a[0:shape[0]]
        return mkV(a, key if key is not None else name)

    def mark(self):
        return self.off

    def reset(self, m):
        self.off = m


D = 1024
NKV = 4352
NKT = 34


def build_program():
    nc = bass.Bass("TRN2", target_bir_lowering=False)

    def din(name, shape, dt=F32):
        return nc.dram_tensor(name, list(shape), dt, kind="ExternalInput").ap()

    x_kv = din("x_kv", [NKV, D])
    ccol_d = din("ccol", [128, 16])
    ada_w_d = din("ada_w", [2, D, 6144])
    ada_b_d = din("ada_b", [1, 2 * 6144])
    ng_d = din("ng", [128, 32])
    w_in_d = din("w_in", [D, 4096])
    qkg_d = din("qkg", [128, 4])
    cos_d = din("cosT", [128, NKV])
    sin_d = din("sinT", [128, NKV])
    band_d = din("band", [128, 7 * 4 * 128])
    poolw_d = din("poolw", [64, 4 * 64])
    pscale_d = din("pscale", [64, 4])
    wo_a_d = din("wo_a", [768, D])
    wo_p_d = din("wo_p", [64, 4 * D])
    subg_d = din("subg", [128, 1])
    lam_d = din("lam", [1, 256])
    w1_d = din("ffn_w1", [2, D, 4096])
    w2_d = din("ffn_w2", [2, 4096, D])
    odin_d = din("od_w_in", [D, 4096])
    lng_d = din("sg_ln_g", [1, 2048])
    lnb_d = din("sg_ln_b", [1, 2048])
    sgw_d = din("sg_wT", [128, 8 * 128])
    sgb_d = din("sg_b", [1, 8 * 128])
    odout_d = din("od_w_out", [2048, D])
    ident_d = din("ident", [128, 128])
    bones_d = din("bones", [128, 128])
    out_d = nc.dram_tensor("out", [2048, D], F32, kind="ExternalOutput").ap()
    dbg_d = nc.dram_tensor("dbg", [4, 2048, D], F32, kind="ExternalOutput").ap() if DEBUG else None
    KTd = nc.dram_tensor("KTd", [6, 128, NKV], BF16, kind="Internal").ap()
    QTd = nc.dram_tensor("QTd", [6, 128, 2048], BF16, kind="Internal").ap()
    Vd = nc.dram_tensor("Vd", [NKV, 768], BF16, kind="Internal").ap()

    st = ExitStack()
    NW = 51 * 1024 + 512
    arena_t = st.enter_context(nc.sbuf_tensor("arena", [128, NW], F32))
    ps_t = st.enter_context(nc.psum_tensor("ps", [128, 8, 512], F32))
    A = Arena(arena_t[:, :], NW)
    P = Prog(nc)

    def bank(b):
        return mkV(ps_t[:, b, :], ("ps", b))

    def banks(b, n):
        return mkV(ps_t[:, b:b + n, :], *[("ps", b + i) for i in range(n)])

    def dr(ap, key=None):
        return mkV(ap, key)

    identb = A.alloc("identb", [128, 128], BF16)
    bonesb = A.alloc("bonesb", [128, 128], BF16)
    onesb = A.alloc("onesb", [128, 128], BF16)
    onesf = A.alloc("onesf", [128, 128], F32)
    ng = A.alloc("ng", [128, 32], F32)
    ccol = A.alloc("ccol", [128, 16], F32)
    scol = A.alloc("scol", [128, 16], F32)
    scolb = A.alloc("scolb", [128, 16], BF16)
    qkg = A.alloc("qkg", [128, 4], F32)
    subg = A.alloc("subg", [128, 1], F32)
    subg2 = A.alloc("subg2", [128, 1], F32)
    neglam = A.alloc("neglam", [128, 1], F32)
    lamrow = A.alloc("lamrow", [1, 256], F32)
    lamt = A.alloc("lamt", [1, 8], F32)
    pscale = A.alloc("pscale", [64, 4], F32)
    modcols = {}
    for l in range(2):
        for nm in ("G1", "SH1", "G2", "SH2"):
            modcols[(l, nm)] = A.alloc("mc_%d_%s" % (l, nm), [128, 8], F32)
    modcols[("c", "G1")] = A.alloc("mc_c_G1", [128, 8], F32)
    modcols[("c", "SH1")] = A.alloc("mc_c_SH1", [128, 8], F32)
    gate = {}
    for l in range(2):
        for nm in ("g1", "g2"):
            gate[(l, nm)] = A.alloc("gate_%d_%s" % (l, nm), [128, 1024], F32)
    tmpcol = A.alloc("tmpcol", [128, 8], F32)
    NST = 4
    st_ssq = [A.alloc("ssq%d" % i, [128, 1], F32) for i in range(NST)]
    st_std = [A.alloc("std%d" % i, [128, 1], F32) for i in range(NST)]
    st_rstd = [A.alloc("rstd%d" % i, [128, 1], F32) for i in range(NST)]
    junk = A.alloc("junk", [128, 1024], BF16)
    junk2 = A.alloc("junk2", [128, 512], BF16)
    xs_buf = [A.alloc("xs%d" % i, [128, 1024], BF16) for i in range(4)]
    X1W = 16 * 1024
    x1 = [mkV(arena_t[:, NW - X1W + j * 1024: NW - X1W + (j + 1) * 1024], "x1_%d" % j) for j in range(16)]
    g_end = A.mark()
    u_store = [A.alloc("u_%d" % j, [128, 256], F32) for j in range(18)]
    u_end = A.mark()

    P.dma(identb, dr(ident_d), "c_ident", eng="pool")
    P.dma(bonesb, dr(bones_d), "c_bones", eng="pool")
    P.dma(ng, dr(ng_d), "c_ng")
    P.dma(ccol, dr(ccol_d), "c_ccol")
    P.dma(qkg, dr(qkg_d), "c_qkg")
    P.dma(subg, dr(subg_d), "c_subg")
    P.dma(lamrow, dr(lam_d), "c_lam")
    P.dma(pscale, dr(pscale_d), "c_pscale")
    P.memset(onesb, 1.0)
    P.memset(onesf, 1.0)
    P.act(scol, ccol, AF.Silu)
    P.copy(scolb, scol)

    LAM_INIT = 0.8 - 0.6 * math.exp(-0.3 * 0)
    P.tt(lamrow[:, 0:64], lamrow[:, 0:64], lamrow[:, 64:128], ALU.mult)
    P.tt(lamrow[:, 128:192], lamrow[:, 128:192], lamrow[:, 192:256], ALU.mult)
    P.op("dve", lambda e: e.tensor_reduce(lamt[:, 0:1].ap, lamrow[:, 0:64].ap, AX.X, ALU.add), [lamrow], [lamt])
    P.op("dve", lambda e: e.tensor_reduce(lamt[:, 1:2].ap, lamrow[:, 128:192].ap, AX.X, ALU.add), [lamrow], [lamt])
    P.act(lamt[:, 2:4], lamt[:, 0:2], AF.Exp)
    P.tt(lamt[:, 4:5], lamt[:, 3:4], lamt[:, 2:3], ALU.subtract)
    P.ts(lamt[:, 5:6], lamt[:, 4:5], -LAM_INIT, None, ALU.add)
    P.mm(bank(0)[:, 0:1], onesf[0:1, :], lamt[:, 5:6])
    P.copy(neglam, bank(0)[:, 0:1])
    P.ts(subg2, subg, 1.0 - LAM_INIT, None, ALU.mult)

    def adaln(l):
        m = A.mark()
        modrow = A.alloc("modrow", [1, 6144], F32)
        ctxrow = A.alloc("ctxrow", [1, 2048], F32)
        adab = A.alloc("adab", [1, 6144], F32)
        awb = [A.alloc("awb%d" % i, [128, 8, 512], BF16) for i in range(3)]
        P.dma(adab, dr(ada_b_d[:, l * 6144:(l + 1) * 6144]), "adab")
        for nb in range(12):
            wb = awb[nb % 3]
            P.dma(wb, dr(ada_w_d[l].rearrange("(c p) n -> p c n", p=128)[:, :, nb * 512:(nb + 1) * 512]),
                  "awb%d" % (nb % 3), eng="pool")
            for k in range(8):
                P.mm(bank(0)[0:1, :], scolb[:, k:k + 1], wb[:, k, :], start=(k == 0), stop=(k == 7))
            P.tt(modrow[:, nb * 512:(nb + 1) * 512], bank(0)[0:1, :], adab[:, nb * 512:(nb + 1) * 512], ALU.add)
            if l == 0 and nb < 4:
                for k in range(8):
                    P.mm(bank(1)[0:1, :], scolb[:, 8 + k:9 + k], wb[:, k, :], start=(k == 0), stop=(k == 7))
                P.tt(ctxrow[:, nb * 512:(nb + 1) * 512], bank(1)[0:1, :], adab[:, nb * 512:(nb + 1) * 512], ALU.add)

        def cols(row, v, dst_sh, dst_g, gcol0):
            pb = bank(2)
            for k in range(8):
                P.mm(pb[:, k:k + 1], row[0:1, v * 1024 + k * 128: v * 1024 + (k + 1) * 128], onesf[0:1, 0:1])
                P.mm(pb[:, 8 + k:9 + k], row[0:1, (v + 1) * 1024 + k * 128:(v + 1) * 1024 + (k + 1) * 128],
                     onesf[0:1, 0:1])
            P.copy(dst_sh, pb[:, 0:8])
            P.ts(tmpcol, pb[:, 8:16], 1.0, None, ALU.add)
            P.tt(dst_g, tmpcol, ng[:, gcol0:gcol0 + 8], ALU.mult)

        cols(modrow, 0, modcols[(l, "SH1")], modcols[(l, "G1")], l * 16)
        cols(modrow, 3, modcols[(l, "SH2")], modcols[(l, "G2")], l * 16 + 8)
        if l == 0:
            cols(ctxrow, 0, modcols[("c", "SH1")], modcols[("c", "G1")], 0)
        for nm, v in (("g1", 2), ("g2", 5)):
            for half in range(2):
                P.mm(bank(3), onesf[0:1, :], modrow[0:1, v * 1024 + half * 512: v * 1024 + (half + 1) * 512])
                P.copy(gate[(l, nm)][:, half * 512:(half + 1) * 512], bank(3))
        P.barrier()
        A.reset(m)

    stat_i = [0]

    def norm_a(xt):
        i = stat_i[0] % NST
        stat_i[0] += 1
        ssq, std, rstd = st_ssq[i], st_std[i], st_rstd[i]
        xs = xs_buf[i]
        P.act(junk, xt, AF.Square, accum_out=ssq)
        P.act(std, ssq, AF.Sqrt, scale=1.0 / D, bias=EPS)
        P.recip(rstd, std)
        P.act(xs, xt, AF.Copy, scale=rstd)
        return xs

    def norm_b(xs, Gc, SHc, hT, col0, trb):
        for half in range(2):
            pb = bank(trb + half)
            for cc in range(4):
                c = half * 4 + cc
                P.mm(pb[:, cc * 128:(cc + 1) * 128], xs[:, c * 128:(c + 1) * 128], identb)
            for cc in range(4):
                c = half * 4 + cc
                P.ts(hT[:, c, col0:col0 + 128], pb[:, cc * 128:(cc + 1) * 128], Gc[:, c:c + 1], SHc[:, c:c + 1],
                     ALU.mult, ALU.add)

    class WStream:
        def __init__(self, nslots):
            self.slots = [A.alloc("wslot%d" % i, [128, 8, 512], BF16) for i in range(nslots)]
            self.n = nslots
            self.plan = []
            self.issued = 0
            self.ptr = 0
            self.g0 = 0

        def set_plan(self, plan):
            assert self.ptr == len(self.plan) and self.issued == len(self.plan)
            self.g0 += len(self.plan)
            self.plan = list(plan)
            self.issued = 0
            self.ptr = 0

        def get(self):
            upto = min(self.ptr + self.n, len(self.plan))
            while self.issued < upto:
                g = self.issued
                kc, src = self.plan[g]
                si = (self.g0 + g) % self.n
                P.dma(self.slots[si][:, 0:kc, :], dr(src), "wslot%d" % si, eng="pool")
                self.issued += 1
            g = self.ptr
            self.ptr += 1
            return self.slots[(self.g0 + g) % self.n]

    def wsrc(w_ap, row0, nk, col0):
        return w_ap.rearrange("(c p) n -> p c n", p=128)[:, row0:row0 + nk, col0:col0 + 512]

    def resid_add(tile_j, half, pbank, gb):
        xv = x1[tile_j][:, half * 512:(half + 1) * 512]
        tmp = rtmp[(tile_j * 2 + half) % len(rtmp)]
        P.tt(tmp, pbank, gb[:, half * 512:(half + 1) * 512], ALU.mult)
        P.tt(xv, xv, tmp, ALU.add, eng=POOL)

    def dump(idx):
        if DEBUG:
            for j in range(16):
                P.dma(dr(dbg_d[idx, j * 128:(j + 1) * 128, :]), x1[j], "dbg%d" % (j % 4), final=True)

    def finish():
        P.barrier()
        for j in range(16):
            P.dma(dr(out_d[j * 128:(j + 1) * 128, :]), x1[j], "out%d" % (j % 4), final=True)
        P.build(st)
        return nc, st, P, A

    adaln(0)
    adaln(1)
    if STOP == 0:
        return finish()

    A.reset(u_end)
    w_in = A.alloc("w_in", [128, 8, 4096], BF16)
    cosT = A.alloc("cosT", [128, NKV], F32)
    sinT = A.alloc("sinT", [128, NKV], F32)
    xt_buf = [A.alloc("xt%d" % i, [128, 1024], F32) for i in range(4)]
    hT_buf = [A.alloc("hT%d" % i, [128, 8, 512], BF16) for i in range(2)]
    sq_buf = [A.alloc("sq%d" % i, [128, 512], BF16) for i in range(2)]
    std_buf = [A.alloc("stdb%d" % i, [128, 512], F32) for i in range(2)]
    r_buf = [A.alloc("rb%d" % i, [128, 512], F32) for i in range(2)]
    t1_buf = [A.alloc("t1b%d" % i, [128, 512], F32) for i in range(2)]
    t2_buf = [A.alloc("t2b%d" % i, [128, 512], F32) for i in range(2)]
    kst_buf = [A.alloc("kst%d" % i, [128, 512], BF16) for i in range(3)]
    vst_buf = [A.alloc("vst%d" % i, [128, 768], BF16) for i in range(2)]

    for g in range(8):
        P.dma(w_in[:, :, g * 512:(g + 1) * 512], dr(wsrc(w_in_d, 0, 8, g * 512)), "w_in%d" % (g % 4), eng="pool")
    if "nocos" not in KF:
        P.dma(cosT, dr(cos_d), "cosT")
        P.dma(sinT, dr(sin_d), "sinT")

    unit_i = [0]
    pending = [None]
    xt_i = [0]

    def qk_unit(h, which, hT, tb, ntok):
        ui = unit_i[0]
        unit_i[0] += 1
        s = ui % 2
        pk, pks = bank(3 + 2 * s), bank(4 + 2 * s)
        cbase = 1024 + h * 512 + which * 256
        for c in range(8):
            P.mm(pk[:, 0:ntok], w_in[:, c, cbase:cbase + 128], hT[:, c, 0:ntok], start=(c == 0), stop=(c == 7))
        for c in range(8):
            P.mm(pks[:, 0:ntok], w_in[:, c, cbase + 128:cbase + 256], hT[:, c, 0:ntok], start=(c == 0), stop=(c == 7))
        sq, stdb, rb, t1, t2 = sq_buf[s], std_buf[s], r_buf[s], t1_buf[s], t2_buf[s]
        P.act(sq[:, 0:ntok], pk[:, 0:ntok], AF.Square)
        gcol, gscol = (qkg[:, 0:1], qkg[:, 1:2]) if which == 0 else (qkg[:, 2:3], qkg[:, 3:4])
        c0 = tb * 512
        P.stt(t1[:, 0:ntok], pk[:, 0:ntok], gcol, cosT[:, c0:c0 + ntok], ALU.mult, ALU.mult)
        P.stt(t2[:, 0:ntok], pks[:, 0:ntok], gscol, sinT[:, c0:c0 + ntok], ALU.mult, ALU.mult)
        P.tt(t1[:, 0:ntok], t1[:, 0:ntok], t2[:, 0:ntok], ALU.add)

        def epi():
            P.mm(bank(1)[:, 0:ntok], bonesb, sq[:, 0:ntok])
            P.act(stdb[:, 0:ntok], bank(1)[:, 0:ntok], AF.Sqrt, scale=1.0 / 64, bias=EPS)
            P.recip(rb[:, 0:ntok], stdb[:, 0:ntok])
            kst = kst_buf[ui % 3]
            P.tt(kst[:, 0:ntok], t1[:, 0:ntok], rb[:, 0:ntok], ALU.mult)
            if which == 0:
                P.dma(dr(QTd[h][:, c0:c0 + ntok], ("QTd", h)), kst[:, 0:ntok], "kst%d" % (ui % 3))
            else:
                P.dma(dr(KTd[h][:, c0:c0 + ntok], ("KTd", h)), kst[:, 0:ntok], "kst%d" % (ui % 3))

        if pending[0] is not None:
            pending[0]()
        pending[0] = epi

    def p1_blk(tb):
        ntile = 4 if tb < 8 else 2
        Gc, SHc = (modcols[(0, "G1")], modcols[(0, "SH1")]) if tb < 8 else (modcols[("c", "G1")], modcols[("c", "SH1")])
        return ntile, ntile * 128, hT_buf[tb % 2], Gc, SHc

    p1_xs = {}

    def p1_norm_a(tb):
        ntile, ntok, hT, Gc, SHc = p1_blk(tb)
        for j in range(ntile):
            kvt = tb * 4 + j
            xt = xt_buf[xt_i[0] % 4]
            P.dma(xt, dr(x_kv[kvt * 128:(kvt + 1) * 128, :]), "xt%d" % (xt_i[0] % 4))
            xt_i[0] += 1
            p1_xs[(tb, j)] = norm_a(xt)

    def p1_norm_b(tb):
        ntile, ntok, hT, Gc, SHc = p1_blk(tb)
        for j in range(ntile):
            norm_b(p1_xs[(tb, j)], Gc, SHc, hT, j * 128, 0)

    p1_norm_a(0)
    p1_norm_b(0)
    for tb in range(9):
        ntile, ntok, hT, Gc, SHc = p1_blk(tb)
        for j in range(ntile):
            kvt = tb * 4 + j
            vst = vst_buf[kvt % 2]
            pb, pb2 = bank(2), bank(7)
            for c in range(8):
                P.mm(pb, hT[:, c, j * 128:(j + 1) * 128], w_in[:, c, 0:512], start=(c == 0), stop=(c == 7))
            for c in range(8):
                P.mm(pb2, hT[:, c, j * 128:(j + 1) * 128], w_in[:, c, 512:1024], start=(c == 0), stop=(c == 7))
            uslot = kvt if kvt < 16 else (17 if kvt == 16 else (16 if kvt == 31 else None))
            if uslot is not None:
                P.copy(u_store[uslot], pb[:, 0:256])
            P.copy(vst[:, 0:256], pb[:, 256:512], eng="act")
            P.copy(vst[:, 256:768], pb2, eng="act")
            P.dma(dr(Vd[kvt * 128:(kvt + 1) * 128, :], ("Vd", kvt)), vst, "vst%d" % (kvt % 2))
        units = []
        for h in range(6):
            if tb < 4:
                units.append((h, 0))
            units.append((h, 1))
        for idx, (h, which) in enumerate(units):
            qk_unit(h, which, hT, tb, ntok)
            if idx == 0 and tb + 1 < 9:
                p1_norm_a(tb + 1)
            if idx == min(len(units) - 2, 5) and tb + 1 < 9:
                p1_norm_b(tb + 1)
    if pending[0] is not None:
        pending[0]()
    pending[0] = None
    P.barrier()
    if STOP == 1:
        return finish()

    A.reset(u_end)
    attnT = A.alloc("attnT", [128, 6, 2048], BF16)
    m2 = A.mark()
    kt_buf = [A.alloc("ktb%d" % i, [128, NKV], BF16) for i in range(2)]
    v_buf = [A.alloc("vb%d" % i, [128, NKT, 128], BF16) for i in range(2)]
    qt_buf = [A.alloc("qtb%d" % i, [128, 2048], BF16) for i in range(2)]
    e_buf = [A.alloc("eb%d" % i, [128, 2, 512], BF16) for i in range(4)]
    ep_buf = [A.alloc("ep%d" % i, [128, 2, 512], BF16) for i in range(2)]
    rz = [A.alloc("rz%d" % i, [128, 512], F32) for i in range(2)]
    a_t = [[A.alloc("at%d_%d" % (i, c), [128, 512], F32) for c in range(2)] for i in range(2)]
    asq = [A.alloc("asq%d" % i, [128, 512], BF16) for i in range(2)]
    astd = A.alloc("astd", [128, 512], F32)
    ar = A.alloc("ar", [128, 512], F32)

    def load_head(h):
        s = h % 2
        kvkeys = tuple(("Vd", t) for t in range(NKT))
        P.dma(kt_buf[s], dr(KTd[h], ("KTd", h)), "ktb%d" % s)
        for q4, (t0, t1) in enumerate(((0, 9), (9, 18), (18, 26), (26, 34))):
            P.dma(v_buf[s][:, t0:t1, :],
                  mkV(Vd.rearrange("(t p) e -> p t e", p=128)[:, t0:t1, h * 128:(h + 1) * 128], *kvkeys),
                  "vb%d_%d" % (s, q4))
        P.dma(qt_buf[s], dr(QTd[h], ("QTd", h)), "qtb%d" % s)

    load_head(0)
    ei = 0
    it = 0
    att_pending = [None]
    for h in range(6):
        if h + 1 < 6:
            load_head(h + 1)
        s = h % 2
        ktb, vb, qtb = kt_buf[s], v_buf[s], qt_buf[s]
        for qb in range(4):
            qs = slice(qb * 512, (qb + 1) * 512)
            ip = it % 2
            it += 1
            at0, at1 = a_t[ip]

            def s_mm(kt):
                sset = kt % 2
                ks = slice(kt * 128, (kt + 1) * 128)
                P.mm(bank(2 * sset), ktb[0:64, ks], qtb[0:64, qs])
                P.mm(bank(2 * sset + 1), ktb[64:128, ks], qtb[64:128, qs])

            s_mm(0)
            s_mm(1)
            for kt in range(NKT):
                sset = kt % 2
                E = e_buf[ei % 4]
                ei += 1
                P.act(E, banks(2 * sset, 2), AF.Exp, scale=0.125, bias=-4.0)
                if kt + 2 < NKT:
                    s_mm(kt + 2)
                first, last = (kt == 0), (kt == NKT - 1)
                P.mm(bank(4), vb[:, kt, :], E[:, 0, :], start=first, stop=last)
                P.mm(bank(5), vb[:, kt, :], E[:, 1, :], start=first, stop=last)
                if kt % 2 == 0:
                    Eprev = E
                else:
                    Ep = ep_buf[(kt // 2) % 2]
                    P.tt(Ep, Eprev, E, ALU.add)
                    P.mm(bank(6), onesb, Ep[:, 0, :], start=(kt == 1), stop=last)
                    P.mm(bank(7), onesb, Ep[:, 1, :], start=(kt == 1), stop=last)
                if kt == 0 and att_pending[0] is not None:
                    att_pending[0]()
                    att_pending[0] = None
            P.copy(at0, bank(4))
            P.copy(at1, bank(5))
            P.recip(rz[0], bank(6))
            P.recip(rz[1], bank(7))
            P.tt(at0, at0, rz[0], ALU.mult)
            P.tt(at1, at1, rz[1], ALU.mult)
            P.stt(at0, at1, neglam, at0, ALU.mult, ALU.add)
            P.tt(asq[ip], at0, at0, ALU.mult)

            def epi(h=h, qs=qs, ip=ip, at0=at0):
                P.mm(bank(6), onesb, asq[ip])
                P.act(astd, bank(6), AF.Ln, scale=1.0 / 128, bias=EPS)
                P.act(ar, astd, AF.Exp, scale=-0.5)
                P.stt(attnT[:, h, qs], at0, subg2, ar, ALU.mult, ALU.mult)

            att_pending[0] = epi
    att_pending[0]()
    P.barrier()
    if STOP == 2:
        return finish()

    A.reset(m2)
    A.n = NW - X1W
    band = A.alloc("band", [128, 7, 4, 128], F32)
    poolw = A.alloc("poolw", [64, 4, 64], BF16)
    wo_a = A.alloc("wo_a", [128, 6, 1024], BF16)
    wo_p = A.alloc("wo_p", [64, 4, 1024], BF16)
    pooledT = [A.alloc("pooledT%d" % i, [64, 4, 128], BF16) for i in range(2)]
    ypoolT = [A.alloc("ypoolT%d" % i, [64, 4, 128], BF16) for i in range(2)]
    rtmp = [A.alloc("rtmp%d" % i, [128, 512], F32) for i in range(4)]
    P.dma(band, dr(band_d.rearrange("p (k g t) -> p k g t", k=7, g=4)), "band")
    P.dma(poolw, dr(poolw_d.rearrange("p (g d) -> p g d", g=4)), "poolw", eng="pool")
    P.dma(wo_a, dr(wo_a_d.rearrange("(h p) n -> p h n", p=128)), "wo_a", eng="pool")
    P.dma(wo_p, dr(wo_p_d.rearrange("p (g n) -> p g n", g=4)), "wo_p", eng="pool")
    for j in range(16):
        P.dma(x1[j], dr(x_kv[j * 128:(j + 1) * 128, :]), "x1_%d" % (j % 4))
    for j in range(16):
        pT, yT = pooledT[j % 2], ypoolT[j % 2]
        up = u_store[16] if j == 0 else u_store[j - 1]
        un = u_store[17] if j == 15 else u_store[j + 1]
        kp, km, kn = (0, 1, 4) if j == 0 else ((2, 5, 6) if j == 15 else (2, 3, 4))
        pb = bank(0)
        for g in range(4):
            o = pb[0:64, g * 128:(g + 1) * 128]
            gs = slice(g * 64, (g + 1) * 64)
            P.mm(o, up[:, gs], band[:, kp, g, :], start=True, stop=False)
            P.mm(o, u_store[j][:, gs], band[:, km, g, :], start=False, stop=False)
            P.mm(o, un[:, gs], band[:, kn, g, :], start=False, stop=True)
        P.copy(pT, pb[0:64, :].rearrange("p (g t) -> p g t", g=4))
        pb2 = bank(1)
        for g in range(4):
            P.mm(pb2[0:64, g * 128:(g + 1) * 128], poolw[:, g, :], pT[:, g, :])
        for g in range(4):
            P.ts(yT[:, g, :], pb2[0:64, g * 128:(g + 1) * 128], pscale[:, g:g + 1], None, ALU.mult)
        for half in range(2):
            pby = bank(2 + ((2 * j + half) % 4))
            hs = slice(half * 512, (half + 1) * 512)
            for h in range(6):
                P.mm(pby, attnT[:, h, j * 128:(j + 1) * 128], wo_a[:, h, hs], start=(h == 0), stop=False)
            for g in range(4):
                P.mm(pby, yT[:, g, :], wo_p[:, g, hs], start=False, stop=(g == 3))
            resid_add(j, half, pby, gate[(0, "g1")])
    P.barrier()
    dump(0)
    if STOP == 3:
        return finish()

    def ffn(l, ws, hT_b, hid, relu_t):
        G2, SH2, gb = modcols[(l, "G2")], modcols[(l, "SH2")], gate[(l, "g2")]
        plan = []
        for tb in range(4):
            for hg in range(8):
                plan.append((8, wsrc(w1_d[l], 0, 8, hg * 512)))
            for half in range(2):
                for sg in range(4):
                    plan.append((8, wsrc(w2_d[l], sg * 8, 8, half * 512)))
        ws.set_plan(plan)
        fxs = {}

        def f_norm_a(tb):
            for j in range(4):
                fxs[(tb, j)] = norm_a(x1[tb * 4 + j])

        def f_norm_b(tb):
            for j in range(4):
                norm_b(fxs[(tb, j)], G2, SH2, hT_b[tb % 2], j * 128, 0)

        f_norm_a(0)
        f_norm_b(0)
        for tb in range(4):
            hT = hT_b[tb % 2]
            for hg in range(8):
                wg = ws.get()
                for hc in range(4):
                    pb = bank(2 + (hg * 4 + hc) % 2)
                    for c in range(8):
                        P.mm(pb, wg[:, c, hc * 128:(hc + 1) * 128], hT[:, c, :], start=(c == 0), stop=(c == 7))
                    rt = relu_t[(hg * 4 + hc) % 2]
                    P.act(rt, pb, AF.Relu)
                    P.tt(hid[:, hg * 4 + hc, :], rt, rt, ALU.mult)
                if hg == 1 and tb + 1 < 4:
                    f_norm_a(tb + 1)
                if hg == 5 and tb + 1 < 4:
                    f_norm_b(tb + 1)
            for half in range(2):
                for sg in range(4):
                    wg = ws.get()
                    for j in range(4):
                        for k in range(8):
                            P.mm(bank(4 + j), hid[:, sg * 8 + k, j * 128:(j + 1) * 128], wg[:, k, :],
                                 start=(sg == 0 and k == 0), stop=(sg == 3 and k == 7))
                for j in range(4):
                    resid_add(tb * 4 + j, half, bank(4 + j), gb)

    A.reset(g_end)
    rtmp = [A.alloc("rtmp%d" % i, [128, 512], F32) for i in range(4)]
    hT_b = [A.alloc("hTf%d" % i, [128, 8, 512], BF16) for i in range(2)]
    hid = A.alloc("hid", [128, 32, 512], BF16)
    relu_t = [A.alloc("relut%d" % i, [128, 512], F32) for i in range(2)]
    ws = WStream(6)
    ffn(0, ws, hT_b, hid, relu_t)
    P.barrier()
    dump(1)
    if STOP == 4:
        return finish()

    A.reset(g_end)
    rtmp = [A.alloc("rtmp%d" % i, [128, 512], F32) for i in range(2)]
    hT_b = [A.alloc("hTf%d" % i, [128, 8, 512], BF16) for i in range(2)]
    ms = A.mark()
    uT = A.alloc("uT", [128, 16, 512], BF16)
    v_bf = [A.alloc("v_bf%d" % j, [128, 2048], BF16) for j in range(4)]
    vn = [A.alloc("vn%d" % i, [128, 2048], BF16) for i in range(2)]
    gT = uT
    gelu_t = [A.alloc("gelut%d" % i, [128, 512], F32) for i in range(2)]
    lng = A.alloc("lng", [128, 2048], BF16)
    lnb = A.alloc("lnb", [128, 2048], BF16)
    sgw = A.alloc("sgw", [128, 8, 128], BF16)
    sgb = A.alloc("sgb", [1, 8, 128], F32)
    lst = A.alloc("lst", [128, 4, 8], F32)
    lst2 = A.alloc("lst2", [128, 4, 8], F32)
    ws = WStream(4)
    P.dma(lng, dr(lng_d.broadcast_to([128, 2048])), "lng", eng="pool")
    P.dma(lnb, dr(lnb_d.broadcast_to([128, 2048])), "lnb", eng="pool")
    P.dma(sgw, dr(sgw_d.rearrange("p (g m) -> p g m", g=8)), "sgw", eng="pool")
    P.dma(sgb, dr(sgb_d.rearrange("p (g m) -> p g m", g=8)), "sgb")
    G1, SH1, gb1 = modcols[(1, "G1")], modcols[(1, "SH1")], gate[(1, "g1")]
    plan = []
    for tb in range(4):
        for ug in range(4):
            plan.append((8, wsrc(odin_d, 0, 8, ug * 512)))
        for vg in range(4):
            plan.append((8, wsrc(odin_d, 0, 8, 2048 + vg * 512)))
        for half in range(2):
            for sg in range(2):
                plan.append((8, wsrc(odout_d, sg * 8, 8, half * 512)))
    ws.set_plan(plan)
    gi = 0
    sxs = {}

    def s_norm_a(tb):
        for j in range(4):
            sxs[(tb, j)] = norm_a(x1[tb * 4 + j])

    def s_norm_b(tb):
        for j in range(4):
            norm_b(sxs[(tb, j)], G1, SH1, hT_b[tb % 2], j * 128, 0)

    s_norm_a(0)
    s_norm_b(0)
    for tb in range(4):
        hT = hT_b[tb % 2]
        for ug in range(4):
            wg = ws.get()
            for hc in range(4):
                pb = bank(2 + (ug * 4 + hc) % 2)
                for c in range(8):
                    P.mm(pb, wg[:, c, hc * 128:(hc + 1) * 128], hT[:, c, :], start=(c == 0), stop=(c == 7))
                P.act(uT[:, ug * 4 + hc, :], pb, AF.Gelu)
        for vg in range(4):
            wg = ws.get()
            if vg == 0 and tb + 1 < 4:
                s_norm_a(tb + 1)
            if vg == 3 and tb + 1 < 4:
                s_norm_b(tb + 1)
            for j in range(4):
                pb = bank(2 + (vg * 4 + j) % 2)
                for c in range(8):
                    P.mm(pb, hT[:, c, j * 128:(j + 1) * 128], wg[:, c, :], start=(c == 0), stop=(c == 7))
                gt = gelu_t[gi % 2]
                gi += 1
                P.act(gt, pb, AF.Gelu, accum_out=lst[:, j, vg:vg + 1])
                P.act(junk2, gt, AF.Square, accum_out=lst[:, j, 4 + vg:5 + vg])
                P.copy(v_bf[j][:, vg * 512:(vg + 1) * 512], gt)
        for j in range(4):
            P.op("dve", lambda e, j=j: e.tensor_reduce(lst2[:, j, 0:1].ap, lst[:, j, 0:4].ap, AX.X, ALU.add), [lst], [lst2])
            P.op("dve", lambda e, j=j: e.tensor_reduce(lst2[:, j, 1:2].ap, lst[:, j, 4:8].ap, AX.X, ALU.add), [lst], [lst2])
            P.ts(lst2[:, j, 2:3], lst2[:, j, 0:1], 1.0 / 2048, None, ALU.mult)
            P.tt(lst2[:, j, 3:4], lst2[:, j, 2:3], lst2[:, j, 2:3], ALU.mult)
            P.stt(lst2[:, j, 4:5], lst2[:, j, 1:2], 1.0 / 2048, lst2[:, j, 3:4], ALU.mult, ALU.subtract)
            P.act(lst2[:, j, 5:6], lst2[:, j, 4:5], AF.Sqrt, bias=EPS)
            P.recip(lst2[:, j, 6:7], lst2[:, j, 5:6])
            vv = vn[j % 2]
            P.ts(vv, v_bf[j], lst2[:, j, 2:3], lst2[:, j, 6:7], ALU.subtract, ALU.mult)
            P.tt(vv, vv, lng, ALU.mult)
            P.tt(vv, vv, lnb, ALU.add)
            for q4 in range(4):
                pb = bank(4 + q4)
                for cc4 in range(4):
                    cc = q4 * 4 + cc4
                    g = cc // 2
                    o = pb[:, cc4 * 128:(cc4 + 1) * 128]
                    P.mm(o, vv[:, cc * 128:(cc + 1) * 128], sgw[:, g, :], start=True, stop=False)
                    P.mm(o, onesf[0:1, :], sgb[:, g, :], start=False, stop=True)
                P.tt(gT[:, q4 * 4:(q4 + 1) * 4, j * 128:(j + 1) * 128],
                     uT[:, q4 * 4:(q4 + 1) * 4, j * 128:(j + 1) * 128],
                     pb.rearrange("p (c m) -> p c m", c=4), ALU.mult)
        for half in range(2):
            for sg in range(2):
                wg = ws.get()
                for j in range(4):
                    for k in range(8):
                        P.mm(bank(j), gT[:, sg * 8 + k, j * 128:(j + 1) * 128], wg[:, k, :],
                             start=(sg == 0 and k == 0), stop=(sg == 1 and k == 7))
            for j in range(4):
                resid_add(tb * 4 + j, half, bank(j), gb1)
    P.barrier()
    dump(2)
    if STOP == 5:
        return finish()

    A.reset(ms)
    rtmp = rtmp + [A.alloc("rtmp%d" % i, [128, 512], F32) for i in (2, 3)]
    hid = A.alloc("hid", [128, 32, 512], BF16)
    relu_t = [A.alloc("relut%d" % i, [128, 512], F32) for i in range(2)]
    ws = WStream(6)
    ffn(1, ws, hT_b, hid, relu_t)
    return finish()


_CACHE = {}


def _rope_tables():
    t = np.arange(4096)
    rows = (t // 64).astype(np.float32)
    cols = (t % 64).astype(np.float32)
    inv = (np.float32(10000.0) ** (-np.arange(16, dtype=np.float32) / np.float32(16))).astype(np.float32)
    ang = np.concatenate([rows[:, None] * inv, cols[:, None] * inv], axis=-1).astype(np.float32)
    return np.cos(ang).astype(np.float32), np.sin(ang).astype(np.float32)


def _band(w, n, T0, kind_rel):
    M = np.zeros((128, 128), np.float32)
    for i in range(128):
        t = T0 + i
        lo = min(max(t - w // 2, 0), n)
        hi = min(max(t + w // 2, 0), n)
        cnt = float(hi - lo)
        for tp in range(lo, hi):
            ip = tp - (T0 + kind_rel * 128)
            if 0 <= ip < 128:
                M[ip, i] += np.float32(1.0) / np.float32(cnt)
        if kind_rel == 0:
            M[i, i] -= 1.0
    return M


def _bands(hf):
    wins = (2, 4, 8, 16)
    n = 4096
    out = np.zeros((128, 7, 4, 128), np.float32)
    Tmid = 1024
    for g, w in enumerate(wins):
        prev = _band(w, n, Tmid, -1)
        mid = _band(w, n, Tmid, 0)
        nxt = _band(w, n, Tmid, 1)
        if hf == 0:
            k0, k1 = np.zeros_like(prev), _band(w, n, 0, 0)
            k5, k6 = mid, nxt
        else:
            k0, k1 = prev, mid
            k5, k6 = _band(w, n, n - 128, 0), np.zeros_like(nxt)
        for k, m in enumerate((k0, k1, prev, mid, nxt, k5, k6)):
            out[:, k, g, :] = m
    return out.reshape(128, 7 * 4 * 128)


def _col(v):
    return np.ascontiguousarray(np.asarray(v, np.float32).reshape(-1, 128).T)


def kernel(x, c, ctx, c_ctx, ada_w, ada_b, norm_mix_g, norm_ffn_g, ffn_w1, ffn_w2,
           ev_w_in, ev_w_out, pool_w, pool_scale, q_norm_g, k_norm_g,
           lam_q1, lam_k1, lam_q2, lam_k2, sub_norm_g,
           od_w_in, sg_ln_g, sg_ln_b, sg_w, sg_b, od_w_out):
    f = lambda a: np.ascontiguousarray(np.asarray(a, dtype=np.float32))
    x, c, ctx, c_ctx = f(x), f(c), f(ctx), f(c_ctx)
    if "nc" not in _CACHE:
        _CACHE["nc"] = build_program()
    nc = _CACHE["nc"][0]

    w_in0 = f(ev_w_in)[0]
    swap = np.arange(128) ^ 1
    cols = [w_in0[:, 0:256], w_in0[:, 1792:2560]]
    for h in range(6):
        q = w_in0[:, 256 + h * 128:256 + (h + 1) * 128]
        k = w_in0[:, 1024 + h * 128:1024 + (h + 1) * 128]
        cols += [q, q[:, swap], k, k[:, swap]]
    w_in_r = np.ascontiguousarray(np.concatenate(cols, axis=1))
    gq, gk = f(q_norm_g)[0], f(k_norm_g)[0]
    d = np.arange(128) % 64
    qkg = np.stack([gq[d], gq[d ^ 1], gk[d], gk[d ^ 1]], axis=1).astype(np.float32)
    cos, sin = _rope_tables()
    pair = d // 2
    sign = np.where(d % 2 == 0, -1.0, 1.0).astype(np.float32)
    cos_full = cos[:, pair].T
    sin_full = (sin[:, pair] * sign[None, :]).T
    ng = np.concatenate([_col(f(norm_mix_g)[0]), _col(f(norm_ffn_g)[0]),
                         _col(f(norm_mix_g)[1]), _col(f(norm_ffn_g)[1])], axis=1)
    w_out0 = f(ev_w_out)[0]
    wo_p = np.ascontiguousarray(w_out0[0:256].reshape(4, 64, 1024).transpose(1, 0, 2).reshape(64, 4096))
    wo_a = np.ascontiguousarray(w_out0[256:1024])
    poolw = np.ascontiguousarray(f(pool_w)[0].transpose(1, 0, 2).reshape(64, 256))
    pscale = np.ascontiguousarray(f(pool_scale)[0].reshape(4, 64).T)
    lam = np.concatenate([f(lam_q1)[0], f(lam_k1)[0], f(lam_q2)[0], f(lam_k2)[0]])[None, :]
    sgwT = np.ascontiguousarray(f(sg_w)[0].transpose(2, 0, 1).reshape(128, 8 * 128))
    sgb = f(sg_b)[0].reshape(1, 8 * 128)
    ident = np.eye(128, dtype=np.float32)
    bones = np.zeros((128, 128), np.float32)
    bones[0:64, 0:64] = 1.0
    bones[64:128, 64:128] = 1.0
    shared = {
        "ada_w": f(ada_w), "ada_b": f(ada_b).reshape(1, -1), "ng": np.ascontiguousarray(ng),
        "w_in": w_in_r, "qkg": qkg, "poolw": poolw, "pscale": pscale, "wo_a": wo_a, "wo_p": wo_p,
        "subg": f(sub_norm_g)[0].reshape(128, 1), "lam": np.ascontiguousarray(lam),
        "ffn_w1": f(ffn_w1), "ffn_w2": f(ffn_w2), "od_w_in": f(od_w_in)[0],
        "sg_ln_g": f(sg_ln_g)[0].reshape(1, -1), "sg_ln_b": f(sg_ln_b)[0].reshape(1, -1),
        "sg_wT": sgwT, "sg_b": sgb, "od_w_out": f(od_w_out)[0], "ident": ident, "bones": bones,
    }
    in_maps = []
    for core in range(8):
        b, hf = core // 2, core % 2
        own = slice(hf * 2048, (hf + 1) * 2048)
        oth = slice((1 - hf) * 2048, (2 - hf) * 2048)
        x_kv = np.concatenate([x[b, own], x[b, oth], ctx[b]], axis=0)
        cosT = np.concatenate([cos_full[:, own], cos_full[:, oth], np.ones((128, 256), np.float32)], axis=1)
        sinT = np.concatenate([sin_full[:, own], sin_full[:, oth], np.zeros((128, 256), np.float32)], axis=1)
        m = dict(shared)
        m["x_kv"] = np.ascontiguousarray(x_kv)
        m["ccol"] = np.ascontiguousarray(np.concatenate([_col(c[b]), _col(c_ctx)], axis=1))
        m["cosT"] = np.ascontiguousarray(cosT)
        m["sinT"] = np.ascontiguousarray(sinT)
        m["band"] = _bands(hf)
        in_maps.append(m)
    import time as _time
    _t0 = _time.time()
    res = run_bass_kernel_spmd(nc, in_maps[:NCORES], core_ids=list(range(NCORES)), **({"trace": True} if _os.environ.get("KTRACE") else {}))
    if _os.environ.get("KTRACE"):
        print("spmd wall", _time.time() - _t0, "exec_time_ns", res.exec_time_ns, flush=True)
    out = np.zeros((4, 4096, 1024), np.float32)
    for core in range(NCORES):
        b, hf = core // 2, core % 2
        out[b, hf * 2048:(hf + 1) * 2048] = res.results[core]["out"]
    if DEBUG:
        _CACHE["dbg"] = [res.results[core]["dbg"] for core in range(NCORES)]
    return out
```
